# Optimizing a Trainium2 kernel written in Bass

```python
import math
import jax
import jax.numpy as jnp
from jax import lax
import numpy as np

D_MODEL = 2048
BATCH = 4
SEQ = 2048
DEPTH = 4

D_MIX = D_MODEL
N_GROUPS = 4
GROUP_W = D_MIX // N_GROUPS
HEAD_DIM = 128
N_HEADS = GROUP_W // HEAD_DIM
DIFF_QK_DIM = HEAD_DIM // 2
MOBA_BLOCK = 256
MOBA_TOPK = 3
MOBA_Q_CHUNK = 64
GLA_DK = HEAD_DIM // 2
GLA_DV = HEAD_DIM
GLA_RANK = 16
GLA_TAU = 16.0
GLA_CHUNK = 64
Q_BLOCK = 128
ROPE_THETA = 500000.0
ROPE_FRACTION = 4
RMS_EPS = 1e-6
FOX_F_BIAS_CENTER = 3.0

SEGMENTS = (
    ("fox_q", GROUP_W), ("fox_k", GROUP_W), ("fox_v", GROUP_W),
    ("fox_f", N_HEADS), ("fox_g", GROUP_W),
    ("diff_q", GROUP_W), ("diff_k", GROUP_W), ("diff_v", GROUP_W), ("diff_g", GROUP_W),
    ("moba_q", GROUP_W), ("moba_k", GROUP_W), ("moba_v", GROUP_W), ("moba_g", GROUP_W),
    ("gla_q", N_HEADS * GLA_DK), ("gla_k", N_HEADS * GLA_DK), ("gla_v", N_HEADS * GLA_DV),
    ("gla_a", GLA_RANK), ("gla_g", GROUP_W),
)
D_IN = sum(w for _, w in SEGMENTS)

kernel_name = "hymba_fox_diff_moba_gla_block"


def rms_norm(x, g):
    xf = x.astype(jnp.float32)
    y = xf * lax.rsqrt(jnp.mean(xf * xf, axis=-1, keepdims=True) + RMS_EPS)
    return (y * g.astype(jnp.float32)).astype(x.dtype)


def split_columns(proj):
    parts, off = {}, 0
    for name, width in SEGMENTS:
        parts[name] = proj[..., off:off + width]
        off += width
    return parts


def to_heads(t, d):
    b, s, _ = t.shape
    return t.reshape(b, s, -1, d).transpose(0, 2, 1, 3)


def from_heads(t):
    b, h, s, d = t.shape
    return t.transpose(0, 2, 1, 3).reshape(b, s, h * d)


def diff_pair(t):
    b, s, _ = t.shape
    t = t.reshape(b, s, N_HEADS, 2, DIFF_QK_DIM).transpose(3, 0, 2, 1, 4)
    return t[0], t[1]


def query_blocks(t, block):
    b, h, s, d = t.shape
    return t.reshape(b, h, s // block, block, d).transpose(2, 0, 1, 3, 4)


def merge_query_blocks(o):
    n, b, h, blk, d = o.shape
    return o.transpose(1, 2, 0, 3, 4).reshape(b, h, n * blk, d)


def rope_table(seq, rot_dim):
    inv_freq = ROPE_THETA ** (-jnp.arange(0, rot_dim, 2, dtype=jnp.float32) / rot_dim)
    ang = jnp.arange(seq, dtype=jnp.float32)[:, None] * inv_freq[None, :]
    return jnp.cos(ang), jnp.sin(ang)


def partial_rope(t, cos, sin):
    half = cos.shape[-1]
    rot = 2 * half
    t1, t2, rest = t[..., :half], t[..., half:rot], t[..., rot:]
    c = cos.astype(t.dtype)
    s = sin.astype(t.dtype)
    return jnp.concatenate([t1 * c - t2 * s, t1 * s + t2 * c, rest], axis=-1)


def forgetting_attention(q, k, v, log_f):
    b, h, s_len, d = q.shape
    cum = jnp.cumsum(log_f.astype(jnp.float32), axis=-1)
    n_blk = s_len // Q_BLOCK
    cum_blocks = cum.reshape(b, h, n_blk, Q_BLOCK).transpose(2, 0, 1, 3)
    k_pos = jnp.arange(s_len)
    scale = d ** -0.5

    def block(args):
        i, q_i, c_i = args
        logits = jnp.einsum("bhqd,bhkd->bhqk", q_i, k, preferred_element_type=jnp.float32) * scale
        logits = logits + c_i[..., :, None] - cum[:, :, None, :]
        q_pos = i * Q_BLOCK + jnp.arange(Q_BLOCK)
        logits = jnp.where(k_pos[None, :] <= q_pos[:, None], logits, -jnp.inf)
        p = jax.nn.softmax(logits, axis=-1)
        return jnp.einsum("bhqk,bhkd->bhqd", p.astype(v.dtype), v)

    out = lax.map(block, (jnp.arange(n_blk), query_blocks(q, Q_BLOCK), cum_blocks))
    return merge_query_blocks(out)


def differential_attention(q1, q2, k1, k2, v, lam):
    s_len, d = q1.shape[2], q1.shape[3]
    n_blk = s_len // Q_BLOCK
    k_pos = jnp.arange(s_len)
    scale = d ** -0.5

    def block(args):
        i, q1_i, q2_i = args
        q_pos = i * Q_BLOCK + jnp.arange(Q_BLOCK)
        causal = k_pos[None, :] <= q_pos[:, None]

        def probs(q_i, k_full):
            logits = jnp.einsum("bhqd,bhkd->bhqk", q_i, k_full, preferred_element_type=jnp.float32) * scale
            return jax.nn.softmax(jnp.where(causal, logits, -jnp.inf), axis=-1)

        p = probs(q1_i, k1) - lam * probs(q2_i, k2)
        return jnp.einsum("bhqk,bhkd->bhqd", p.astype(v.dtype), v)

    out = lax.map(block, (jnp.arange(n_blk), query_blocks(q1, Q_BLOCK), query_blocks(q2, Q_BLOCK)))
    return merge_query_blocks(out)


def moba_attention(q, k, v):
    b, h, s_len, d = q.shape
    n_blk = -(-s_len // MOBA_BLOCK)
    pad = n_blk * MOBA_BLOCK - s_len
    widths = ((0, 0), (0, 0), (0, pad), (0, 0))
    qp, kp, vp = (jnp.pad(t, widths) for t in (q, k, v))
    k_blocks = kp.reshape(b, h, n_blk, MOBA_BLOCK, d)
    v_blocks = vp.reshape(b, h, n_blk, MOBA_BLOCK, d)
    k_mean = jnp.mean(k_blocks.astype(jnp.float32), axis=3)
    n_sel = min(MOBA_TOPK, n_blk)
    scale = d ** -0.5
    gather = jax.vmap(jax.vmap(lambda blocks, idx: blocks[idx]))

    def chunk(args):
        c, q_c = args
        q_pos = c * MOBA_Q_CHUNK + jnp.arange(MOBA_Q_CHUNK)
        own = (c * MOBA_Q_CHUNK) // MOBA_BLOCK
        gate = jnp.einsum("bhqd,bhnd->bhqn", q_c.astype(jnp.float32), k_mean)
        gate = jnp.where(jnp.arange(n_blk) < own, gate, -jnp.inf)
        _, top_idx = lax.top_k(gate, n_sel)
        sel_ok = jnp.arange(n_sel) < own
        k_sel = gather(k_blocks, top_idx)
        v_sel = gather(v_blocks, top_idx)
        s_sel = jnp.einsum("bhqd,bhqjld->bhqjl", q_c, k_sel, preferred_element_type=jnp.float32) * scale
        s_sel = jnp.where(sel_ok[:, None], s_sel, -jnp.inf).reshape(b, h, MOBA_Q_CHUNK, n_sel * MOBA_BLOCK)
        k_own = lax.dynamic_slice_in_dim(kp, own * MOBA_BLOCK, MOBA_BLOCK, axis=2)
        v_own = lax.dynamic_slice_in_dim(vp, own * MOBA_BLOCK, MOBA_BLOCK, axis=2)
        s_own = jnp.einsum("bhqd,bhld->bhql", q_c, k_own, preferred_element_type=jnp.float32) * scale
        own_pos = own * MOBA_BLOCK + jnp.arange(MOBA_BLOCK)
        s_own = jnp.where(own_pos[None, :] <= q_pos[:, None], s_own, -jnp.inf)
        p = jax.nn.softmax(jnp.concatenate([s_sel, s_own], axis=-1), axis=-1).astype(v.dtype)
        p_sel = p[..., :n_sel * MOBA_BLOCK].reshape(b, h, MOBA_Q_CHUNK, n_sel, MOBA_BLOCK)
        p_own = p[..., n_sel * MOBA_BLOCK:]
        return (jnp.einsum("bhqjl,bhqjld->bhqd", p_sel, v_sel)
                + jnp.einsum("bhql,bhld->bhqd", p_own, v_own))

    n_chunk = (n_blk * MOBA_BLOCK) // MOBA_Q_CHUNK
    out = lax.map(chunk, (jnp.arange(n_chunk), query_blocks(qp, MOBA_Q_CHUNK)))
    return merge_query_blocks(out)[:, :, :s_len]


def gla_chunked(q, k, v, log_a):
    b, h, s_len, dk = q.shape
    dv = v.shape[-1]
    qf = q.astype(jnp.float32) * dk ** -0.5
    kf = k.astype(jnp.float32)
    vf = v.astype(jnp.float32)
    gf = log_a.astype(jnp.float32)
    qc, kc, vc, gc = (query_blocks(t, GLA_CHUNK) for t in (qf, kf, vf, gf))
    causal = jnp.tril(jnp.ones((GLA_CHUNK, GLA_CHUNK), dtype=bool))[..., None]

    def step(state, inp):
        q_i, k_i, v_i, g_i = inp
        cum = jnp.cumsum(g_i, axis=2)
        rel = jnp.where(causal, cum[:, :, :, None, :] - cum[:, :, None, :, :], -jnp.inf)
        scores = jnp.einsum("bhtd,bhsd,bhtsd->bhts", q_i, k_i, jnp.exp(rel))
        o = (jnp.einsum("bhts,bhsv->bhtv", scores, v_i)
             + jnp.einsum("bhtd,bhdv->bhtv", q_i * jnp.exp(cum), state))
        last = cum[:, :, -1:, :]
        state = (jnp.exp(last[:, :, 0, :])[..., None] * state
                 + jnp.einsum("bhsd,bhsv->bhdv", k_i * jnp.exp(last - cum), v_i))
        return state, o

    state0 = jnp.zeros((b, h, dk, dv), jnp.float32)
    _, out = lax.scan(step, state0, (qc, kc, vc, gc))
    return merge_query_blocks(out).astype(v.dtype)


def hybrid_layer(x, layer_idx, norm_g, w_in, fox_fb, diff_lam, diff_norm_g,
                 gla_wa2, gla_ba, gla_norm_g, w_out, rope_diff, rope_moba):
    h = rms_norm(x, norm_g)
    p = split_columns(jnp.einsum("bsd,de->bse", h, w_in))

    log_f = jax.nn.log_sigmoid((p["fox_f"] + fox_fb).astype(jnp.float32)).transpose(0, 2, 1)
    fox = forgetting_attention(to_heads(p["fox_q"], HEAD_DIM), to_heads(p["fox_k"], HEAD_DIM),
                               to_heads(p["fox_v"], HEAD_DIM), log_f)
    fox = from_heads(fox) * jax.nn.silu(p["fox_g"])

    lambda_init = 0.8 - 0.6 * math.exp(-0.3 * layer_idx)
    lf = diff_lam.astype(jnp.float32)
    lam = jnp.exp(jnp.sum(lf[0] * lf[1])) - jnp.exp(jnp.sum(lf[2] * lf[3])) + lambda_init
    q1, q2 = diff_pair(p["diff_q"])
    k1, k2 = diff_pair(p["diff_k"])
    q1, q2, k1, k2 = (partial_rope(t, *rope_diff) for t in (q1, q2, k1, k2))
    diff = differential_attention(q1, q2, k1, k2, to_heads(p["diff_v"], HEAD_DIM), lam)
    diff = rms_norm(diff, diff_norm_g) * (1.0 - lambda_init)
    diff = from_heads(diff) * jax.nn.silu(p["diff_g"])

    mq = partial_rope(to_heads(p["moba_q"], HEAD_DIM), *rope_moba)
    mk = partial_rope(to_heads(p["moba_k"], HEAD_DIM), *rope_moba)
    moba = moba_attention(mq, mk, to_heads(p["moba_v"], HEAD_DIM))
    moba = from_heads(moba) * jax.nn.silu(p["moba_g"])

    a_logit = jnp.einsum("bsr,re->bse", p["gla_a"], gla_wa2) + gla_ba
    log_a = jax.nn.log_sigmoid(a_logit.astype(jnp.float32)) / GLA_TAU
    gla = gla_chunked(to_heads(p["gla_q"], GLA_DK), to_heads(p["gla_k"], GLA_DK),
                      to_heads(p["gla_v"], GLA_DV), to_heads(log_a, GLA_DK))
    gla = from_heads(rms_norm(gla, gla_norm_g)) * jax.nn.silu(p["gla_g"])

    mixed = jnp.concatenate([fox, diff, moba, gla], axis=-1)
    return x + jnp.einsum("bse,ed->bsd", mixed, w_out)


def setup_inputs(seed: int = 0) -> dict:
    key = jax.random.key(seed)
    ks = jax.random.split(key, 11)
    f32 = jnp.float32
    nrm = jax.random.normal
    x = nrm(ks[0], (BATCH, SEQ, D_MODEL), f32)
    norm_g = 1.0 + 0.02 * nrm(ks[1], (DEPTH, D_MODEL), f32)
    w_in = nrm(ks[2], (DEPTH, D_MODEL, D_IN), f32) * D_MODEL ** -0.5
    fox_fb = FOX_F_BIAS_CENTER + 0.1 * nrm(ks[3], (DEPTH, N_HEADS), f32)
    diff_lam = 0.1 * nrm(ks[4], (DEPTH, 4, DIFF_QK_DIM), f32)
    diff_norm_g = 1.0 + 0.02 * nrm(ks[5], (DEPTH, HEAD_DIM), f32)
    gla_wa2 = nrm(ks[6], (DEPTH, GLA_RANK, N_HEADS * GLA_DK), f32) * GLA_RANK ** -0.5
    gla_ba = 0.1 * nrm(ks[7], (DEPTH, N_HEADS * GLA_DK), f32)
    gla_norm_g = 1.0 + 0.02 * nrm(ks[8], (DEPTH, GLA_DV), f32)
    w_out = nrm(ks[9], (DEPTH, D_MIX, D_MODEL), f32) * D_MIX ** -0.5
    final_norm_g = 1.0 + 0.02 * nrm(ks[10], (D_MODEL,), f32)
    return {"x": x, "norm_g": norm_g, "w_in": w_in, "fox_fb": fox_fb, "diff_lam": diff_lam,
            "diff_norm_g": diff_norm_g, "gla_wa2": gla_wa2, "gla_ba": gla_ba,
            "gla_norm_g": gla_norm_g, "w_out": w_out, "final_norm_g": final_norm_g}


def reference(x, norm_g, w_in, fox_fb, diff_lam, diff_norm_g, gla_wa2, gla_ba,
              gla_norm_g, w_out, final_norm_g):
    s_len = x.shape[1]
    rope_diff = rope_table(s_len, DIFF_QK_DIM // ROPE_FRACTION)
    rope_moba = rope_table(s_len, HEAD_DIM // ROPE_FRACTION)
    for l in range(DEPTH):
        x = hybrid_layer(x, l, norm_g[l], w_in[l], fox_fb[l], diff_lam[l], diff_norm_g[l],
                         gla_wa2[l], gla_ba[l], gla_norm_g[l], w_out[l], rope_diff, rope_moba)
    return rms_norm(x, final_norm_g)
```

```python
import math
import numpy as np
import concourse.bass as bass
import concourse.mybir as mybir
from concourse.bass_utils import run_bass_kernel_spmd

F32 = mybir.dt.float32
BF16 = mybir.dt.bfloat16
AF = mybir.ActivationFunctionType
ALU = mybir.AluOpType
AX = mybir.AxisListType

D = 2048
S = 2048
DEPTH = 4
D_IN = 7700
NTB = 16
NKC = 16
RMS_EPS = 1e-6
ROPE_THETA = 500000.0
NEG = -30000.0

OFF = {}
_o = 0
for _n, _w in (("fox_q", 512), ("fox_k", 512), ("fox_v", 512), ("fox_f", 4), ("fox_g", 512),
               ("diff_q", 512), ("diff_k", 512), ("diff_v", 512), ("diff_g", 512),
               ("moba_q", 512), ("moba_k", 512), ("moba_v", 512), ("moba_g", 512),
               ("gla_q", 256), ("gla_k", 256), ("gla_v", 512), ("gla_a", 16), ("gla_g", 512)):
    OFF[_n] = _o
    _o += _w
assert _o == D_IN

BLK = ["fq0", "fq1", "fk0", "fk1", "fg0", "fg1",
       "dq0", "dqs0", "dq1", "dqs1", "dk0", "dks0", "dk1", "dks1", "dg0", "dg1",
       "mq0", "mqs0", "mq1", "mqs1", "mk0", "mks0", "mk1", "mks1", "mg0", "mg1",
       "ga", "gq", "gk", "gg0", "gg1"]
BIDX = {n: i for i, n in enumerate(BLK)}
NBLK = len(BLK)


def _swap_cols_diff():
    m = {}
    for base in (0, 64):
        for c in range(8):
            m[base + c] = base + c + 8
            m[base + 8 + c] = base + c
    return m


def _swap_cols_moba():
    m = {}
    for c in range(16):
        m[c] = c + 16
        m[16 + c] = c
    return m


def _block_cols(hh):
    cols = {}
    for j in range(2):
        h = 2 * hh + j
        r = np.arange(128)
        cols[f"fq{j}"] = OFF["fox_q"] + h * 128 + r
        cols[f"fk{j}"] = OFF["fox_k"] + h * 128 + r
        cols[f"fg{j}"] = OFF["fox_g"] + h * 128 + r
        for pre, seg, sw in (("d", "diff", _swap_cols_diff()), ("m", "moba", _swap_cols_moba())):
            for t in ("q", "k"):
                base = OFF[f"{seg}_{t}"] + h * 128
                cols[f"{pre}{t}{j}"] = base + r
                s = np.full(128, -1, dtype=np.int64)
                for c, src in sw.items():
                    s[c] = base + src
                cols[f"{pre}{t}s{j}"] = s
            cols[f"{pre}g{j}"] = OFF[f"{seg}_g"] + h * 128 + r
        cols[f"gg{j}"] = OFF["gla_g"] + h * 128 + r
    r64 = np.arange(64)
    cols["gq"] = np.concatenate([OFF["gla_q"] + (2 * hh) * 64 + r64, OFF["gla_q"] + (2 * hh + 1) * 64 + r64])
    cols["gk"] = np.concatenate([OFF["gla_k"] + (2 * hh) * 64 + r64, OFF["gla_k"] + (2 * hh + 1) * 64 + r64])
    a = np.full(128, -1, dtype=np.int64)
    a[:16] = OFF["gla_a"] + np.arange(16)
    cols["ga"] = a
    return cols


def pack_layer_weights(w_in_l, hh):
    cols = _block_cols(hh)
    idx = np.stack([cols[n] for n in BLK])
    wz = np.concatenate([w_in_l, np.zeros((D, 1), np.float32)], axis=1)
    idx = np.where(idx < 0, D_IN, idx)
    g = wz[:, idx.reshape(-1)].reshape(NKC, 128, NBLK, 128)
    wblk = np.ascontiguousarray(g.transpose(2, 1, 0, 3))
    vcols = np.stack([OFF[f"{seg}_v"] + hh * 256 + np.arange(256) for seg in ("fox", "diff", "moba", "gla")])
    gv = w_in_l[:, vcols.reshape(-1)].reshape(NKC, 128, 4, 256)
    wv = np.ascontiguousarray(gv.transpose(2, 1, 0, 3))
    fcols = OFF["fox_f"] + 2 * hh + np.arange(2)
    wf = np.ascontiguousarray(w_in_l[:, fcols].reshape(NKC, 128, 2).transpose(1, 0, 2))
    return wblk, wv, wf


def rope_tables():
    t = np.arange(S, dtype=np.float32)
    out = np.zeros((4, 128, S), np.float32)
    out[0] = 1.0
    out[2] = 1.0
    inv = (ROPE_THETA ** (-np.arange(0, 16, 2, dtype=np.float32) / 16)).astype(np.float32)
    ang = t[None, :] * inv[:, None]
    for base in (0, 64):
        out[0, base:base + 8] = np.cos(ang)
        out[0, base + 8:base + 16] = np.cos(ang)
        out[1, base:base + 8] = -np.sin(ang)
        out[1, base + 8:base + 16] = np.sin(ang)
    inv = (ROPE_THETA ** (-np.arange(0, 32, 2, dtype=np.float32) / 32)).astype(np.float32)
    ang = t[None, :] * inv[:, None]
    out[2, 0:16] = np.cos(ang)
    out[2, 16:32] = np.cos(ang)
    out[3, 0:16] = -np.sin(ang)
    out[3, 16:32] = np.sin(ang)
    return out


def const_tables():
    c = np.zeros((128, 384), np.float32)
    c[:, 0:128] = np.eye(128, dtype=np.float32)
    c[:, 128:256] = np.triu(np.ones((128, 128), np.float32))
    c[:, 256:384] = 1.0
    return c


class Tok:
    __slots__ = ("name", "w", "r", "dsem")

    def __init__(self, name):
        self.name = name
        self.w = None
        self.r = {}
        self.dsem = None


class Sched:
    def __init__(self, nc):
        self.nc = nc
        self.E = {"pe": nc.tensor, "act": nc.scalar, "dve": nc.vector, "pool": nc.gpsimd, "sp": nc.sync}
        self.sems = []
        self.cnt = []
        self.ekey = {}
        for e in ("pe", "act", "dve", "pool"):
            self.ekey[e] = self._newsem("e_" + e)
        self.seen = {e: {} for e in self.E}
        self.ninst = 0

    def _newsem(self, name):
        self.sems.append(self.nc.alloc_semaphore(name=name))
        self.cnt.append(0)
        return len(self.sems) - 1

    def tok(self, name):
        return Tok(name)

    def toks(self, name, n):
        return [Tok(f"{name}{i}") for i in range(n)]

    def _deps(self, R, W):
        deps = {}

        def add(k, v):
            if deps.get(k, 0) < v:
                deps[k] = v
        for t in R:
            if t.w is not None:
                add(*t.w)
        for t in W:
            if t.w is not None:
                add(*t.w)
            for k, v in t.r.items():
                add(k, v)
        return deps

    def _emit_waits(self, e, deps, skip_key=None):
        seen = self.seen[e]
        for k, v in deps.items():
            if k == skip_key:
                continue
            if seen.get(k, 0) < v:
                self.E[e].wait_ge(self.sems[k], v)
                seen[k] = v
                self.ninst += 1

    def op(self, e, fn, R=(), W=()):
        deps = self._deps(R, W)
        key = self.ekey[e]
        self._emit_waits(e, deps, skip_key=key if e == "pe" else None)
        ins = fn(self.E[e])
        ins.then_inc(self.sems[key], 1)
        self.cnt[key] += 1
        v = self.cnt[key]
        self.ninst += 1
        for t in R:
            if t.r.get(key, 0) < v:
                t.r[key] = v
        for t in W:
            t.w = (key, v)
            t.r = {}
        return ins

    def dma(self, q, out, in_, sb, R=(), W=()):
        if sb.dsem is None:
            sb.dsem = self._newsem("d_" + sb.name)
        key = sb.dsem
        deps = self._deps(R, W)
        if self.cnt[key] > 0:
            deps[key] = max(deps.get(key, 0), self.cnt[key])
        self._emit_waits(q, deps)
        ins = self.E[q].dma_start(out=out, in_=in_)
        ins.then_inc(self.sems[key], 16)
        self.cnt[key] += 16
        v = self.cnt[key]
        self.ninst += 1
        for t in R:
            if t.r.get(key, 0) < v:
                t.r[key] = v
        for t in W:
            t.w = (key, v)
            t.r = {}
        return ins

    def coll(self, fn, R=(), W=()):
        if not hasattr(self, "ckey"):
            self.ckey = self._newsem("coll")
        key = self.ckey
        deps = self._deps(R, W)
        if self.cnt[key] > 0:
            deps[key] = max(deps.get(key, 0), self.cnt[key])
        self._emit_waits("pool", deps)
        ins = fn(self.E["pool"])
        ins.then_inc(self.sems[key], 1)
        self.cnt[key] += 1
        v = self.cnt[key]
        self.ninst += 1
        for t in R:
            if t.r.get(key, 0) < v:
                t.r[key] = v
        for t in W:
            t.w = (key, v)
            t.r = {}
        return ins

    def finish(self, toks):
        deps = {}
        for t in toks:
            if t.w is not None and deps.get(t.w[0], 0) < t.w[1]:
                deps[t.w[0]] = t.w[1]
        self._emit_waits("sp", deps)


class SB:
    def __init__(self, nc):
        self.nc = nc
        self.off = 16384
        self.t = {}

    def alloc(self, name, shape, dt, at=None):
        nbytes = int(np.prod(shape[1:])) * (2 if dt == BF16 else 4)
        if at is None:
            at = self.off
            self.off = (at + nbytes + 63) // 64 * 64
        h = self.nc.alloc_sbuf_tensor_at(name, list(shape), dt, offset=at)
        self.t[name] = h
        return h, at, nbytes


class Prog:
    def __init__(self, nc):
        self.nc = nc
        self.s = Sched(nc)
        self.sb = SB(nc)
        self._alloc()

    def _alloc(self):
        sb, s = self.sb, self.s
        A = sb.alloc
        self.hT, self.hT_off, _ = A("hT", [128, NKC, S], BF16)
        self.t_hT = s.tok("hT")
        self.qT, g0, _ = A("qT", [128, 2, S], BF16)
        self.g0 = g0
        self.kT, _, _ = A("kT", [128, 2, S], BF16)
        self.v, _, _ = A("v", [128, NTB, 256], BF16)
        self.sg, _, _ = A("sg", [128, 2, S], BF16)
        self.t_qT, self.t_kT, self.t_v, self.t_sg = s.tok("qT"), s.tok("kT"), s.tok("v"), s.tok("sg")
        self.xt = [sb.alloc(f"xt{i}", [128, D], F32, at=g0 + i * 8192)[0] for i in range(2)]
        self.hrow, _, _ = sb.alloc("hrow", [128, D], BF16, at=g0 + 16384)
        self.junk, _, _ = sb.alloc("junk", [128, D], BF16, at=g0 + 20480)
        self.gbc, _, _ = sb.alloc("gbc", [128, D], F32, at=g0 + 24576)
        self.t_xt = [self.t_qT, self.t_kT]
        self.t_hrow = self.t_v
        self.t_junk = self.t_v
        self.t_gbc = self.t_sg
        self.scr_off = sb.off
        sb.off += 20480
        o = self.scr_off
        self.eg, _, _ = sb.alloc("eg", [128, S], BF16, at=o)
        self.eng, _, _ = sb.alloc("eng", [128, S], BF16, at=o + 4096)
        self.eend, _, _ = sb.alloc("eend", [128, S], BF16, at=o + 8192)
        self.gcs, _, _ = sb.alloc("gcs", [128, S], F32, at=o + 12288)
        self.negB, _, _ = sb.alloc("negB", [128, 8, 8, 128], BF16, at=o)
        self.t_scr = s.tok("scr")
        self.mixT, self.mix_off, _ = A("mixT", [128, 8, S], BF16)
        self.t_mix = s.toks("mix", 8)
        self.NW = 4
        self.wb = [A(f"wb{i}", [128, NKC, 128], BF16)[0] for i in range(self.NW)]
        self.t_wb = s.toks("wb", self.NW)
        self.wvb, _, _ = A("wvb", [128, NKC, 256], BF16)
        self.t_wvb = s.tok("wvb")
        self.wfb, _, _ = A("wfb", [128, NKC, 2], BF16)
        self.t_wfb = s.tok("wfb")
        self.NPT = 4
        self.pt = [A(f"pt{i}", [128, 512], BF16)[0] for i in range(self.NPT)]
        self.t_pt = s.toks("pt", self.NPT)
        self.ft_off = sb.off
        self.ft = [A(f"ft{i}", [128, 512], F32)[0] for i in range(4)]
        self.t_ft = s.toks("ft", 4)
        self.sq, _, _ = A("sq", [128, 512], BF16)
        self.t_sq = s.tok("sq")
        self.rt = [A(f"rt{i}", [128, 2, 512], F32)[0] for i in range(2)]
        self.t_rt = s.toks("rt", 2)
        self.ident, _, _ = A("ident", [128, 128], BF16)
        self.tri, _, _ = A("tri", [128, 128], BF16)
        self.onesb, _, _ = A("onesb", [128, 128], BF16)
        self.negtri, _, _ = A("negtri", [128, 128], BF16)
        self.cf32, _, _ = A("cf32", [128, 384], F32)
        self.t_const = s.tok("const")
        self.epsc, _, _ = A("epsc", [128, 2], F32)
        self.t_out = s.tok("out")
        self.fb, _, _ = A("fb", [128, 2], F32)
        self.lam, _, _ = A("lam", [128, 256], F32)
        self.lsc, _, _ = A("lsc", [128, 8], F32)
        self.dng, _, _ = A("dng", [128, 1], F32)
        self.gng, _, _ = A("gng", [128, 1], F32)
        self.gba, _, _ = A("gba", [128, 1], F32)
        self.wa2, _, _ = A("wa2", [16, 128], BF16)
        self.t_par = s.tok("par")
        self.t_wa2 = s.tok("wa2")
        self.sm, _, _ = A("sm", [128, 64], F32)
        self.t_sm = s.tok("sm")
        self.sm2, _, _ = A("sm2", [128, 512], F32)
        self.t_sm2 = s.tok("sm2")
        self.sm3, _, _ = A("sm3", [128, 256], F32)
        self.t_sm3 = s.tok("sm3")
        self.aT = self.kT[0:16, 1, :]
        self.t_aT = self.t_kT
        self.S32, _, _ = A("S32", [128, 128], F32)
        self.S16, _, _ = A("S16", [128, 2, 128], BF16)
        self.t_S32, self.t_S16 = s.tok("S32"), s.tok("S16")
        self.ktok, _, _ = A("ktok", [128, NTB, 128], BF16)
        self.t_ktok = s.tok("ktok")
        self.at, _, _ = A("at", [128, 2, 128], BF16)
        self.t_at = s.tok("at")
        self.at2, _, _ = A("at2", [128, 2, 128], BF16)
        self.t_at2 = s.tok("at2")
        self.S16b, _, _ = A("S16b", [128, 2, 128], BF16)
        self.t_S16b = s.tok("S16b")
        self.kmT, _, _ = A("kmT", [128, 8], BF16)
        self.t_kmT = s.tok("kmT")
        self.nsel, _, _ = A("nsel", [128, 8, 8], BF16)
        self.t_nsel = s.tok("nsel")
        assert sb.off <= 229376, sb.off
        self.B = [self.nc.alloc_psum_tensor(f"B{i}", [128, 512], F32) for i in range(7)]
        self.t_B = s.toks("B", 7)
        self.T = self.nc.alloc_psum_tensor("T", [128, 1024], BF16)
        self.t_T = s.tok("T")
        self._pb = 0
        self._wi = 0

    def load_consts(self, consts_ap):
        s = self.s
        s.dma("sp", self.cf32[:], consts_ap, self.t_const, W=[self.t_const])
        s.op("dve", lambda e: e.tensor_copy(out=self.ident[:], in_=self.cf32[:, 0:128]), R=[self.t_const], W=[self.t_const])
        s.op("dve", lambda e: e.tensor_copy(out=self.tri[:], in_=self.cf32[:, 128:256]), R=[self.t_const], W=[self.t_const])
        s.op("dve", lambda e: e.tensor_copy(out=self.onesb[:], in_=self.cf32[:, 256:384]), R=[self.t_const], W=[self.t_const])
        s.op("dve", lambda e: e.tensor_scalar(out=self.negtri[:], in0=self.cf32[:, 128:256], scalar1=-1.0, scalar2=-NEG, op0=ALU.add, op1=ALU.mult),
             R=[self.t_const], W=[self.t_const])
        s.op("dve", lambda e: e.memset(self.epsc[:, 0:1], RMS_EPS), W=[self.t_const])
        s.op("dve", lambda e: e.memset(self.epsc[:, 1:2], 1.0), W=[self.t_const])

    def load_params(self, p):
        s = self.s
        t = self.t_par
        s.dma("sp", self.fb[:], p["fb"], t, W=[t])
        s.dma("sp", self.lam[:], p["lam"], t, W=[t])
        s.dma("sp", self.lsc[:], p["lsc"], t, W=[t])
        s.dma("sp", self.dng[:], p["dng"], t, W=[t])
        s.dma("sp", self.gng[:], p["gng"], t, W=[t])
        s.dma("sp", self.gba[:], p["gba"], t, W=[t])
        s.dma("pool", self.wa2[:], p["wa2"], self.t_wa2, W=[self.t_wa2])

    def next_bank(self, n=6):
        i = self._pb % n
        self._pb += 1
        return i

    def norm_phase(self, x_ap, g_ap, t_x=None):
        s = self.s
        if not hasattr(self, "t_st"):
            self.t_st = s.toks("st", 2)
            self.hrows = [self.hrow, self.junk]
            self.t_hrows = [s.tok("hrow0"), s.tok("hrow1")]
            self.junk2 = self.nc.alloc_sbuf_tensor_at("junk2", [128, D], BF16, offset=self.ft_off)
            self.Tb = [self.T[:, :], self.B[5][:, :].bitcast(BF16), self.B[6][:, :].bitcast(BF16)]
            self.t_Tb = [self.t_T, self.t_B[5], self.t_B[6]]
        s.op("dve", lambda e: e.memset(self.sm[:, 21:22], 0.0), W=[self.t_v, self.t_sm] + self.t_hrows)
        s.dma("sp", self.gbc[:], g_ap, self.t_gbc, W=[self.t_gbc])
        nT = 0
        for tb in range(NTB):
            i = tb % 2
            xt, txt = self.xt[i], self.t_xt[i]
            hr, thr = self.hrows[i], self.t_hrows[i]
            st, tst = self.sm[:, 24 + 4 * i:28 + 4 * i], self.t_st[i]
            if callable(x_ap):
                for (cs_, src_, tk_) in x_ap(tb):
                    s.dma("sp", xt[:, cs_], src_, txt, R=[tk_], W=[txt])
            else:
                s.dma("sp", xt[:], x_ap[tb * 128:(tb + 1) * 128, :], txt, R=([t_x] if t_x is not None else []), W=[txt])
            s.op("act", lambda e: e.activation(out=self.junk2[:], in_=xt[:], func=AF.Square, accum_out=st[:, 0:1]),
                 R=[txt], W=[self.t_ft[0], self.t_ft[1], tst])
            s.op("act", lambda e: e.activation(out=st[:, 1:2], in_=st[:, 0:1], func=AF.Sqrt, bias=self.epsc[:, 0:1], scale=1.0 / D),
                 R=[tst, self.t_const], W=[tst])
            s.op("dve", lambda e: e.reciprocal(out=st[:, 2:3], in_=st[:, 1:2]), R=[tst], W=[tst])
            s.op("dve", lambda e: e.scalar_tensor_tensor(out=hr[:], in0=xt[:], scalar=st[:, 2:3], in1=self.gbc[:],
                                                         op0=ALU.mult, op1=ALU.mult),
                 R=[txt, tst, self.t_gbc], W=[thr])
            for half in range(2):
                Tb, tTb = self.Tb[nT % 3], self.t_Tb[nT % 3]
                nT += 1
                for k in range(8):
                    kc = half * 8 + k
                    s.op("pe", lambda e: e.transpose(out=Tb[:, k * 128:(k + 1) * 128], in_=hr[:, kc * 128:(kc + 1) * 128],
                                                     identity=self.ident[:]),
                         R=[thr, self.t_const], W=[tTb])
                src = Tb.rearrange("p (k t) -> p k t", t=128)
                dst = self.hT[:, half * 8:(half + 1) * 8, tb * 128:(tb + 1) * 128]
                if half == 0:
                    s.op("act", lambda e: e.copy(out=dst, in_=src), R=[tTb], W=[self.t_hT])
                else:
                    s.op("dve", lambda e: e.tensor_copy(out=dst, in_=src), R=[tTb], W=[self.t_hT])
        s.op("dve", lambda e: e.memset(self.sm[:, 21:22], 0.0), W=[self.t_v, self.t_sm] + self.t_hrows)

    def load_wblk(self, wblk_ap, name):
        i = self._wi % self.NW
        self._wi += 1
        self.s.dma("pool", self.wb[i][:], wblk_ap[BIDX[name]], self.t_wb[i], W=[self.t_wb[i]])
        return self.wb[i], self.t_wb[i]

    def proj_chunk(self, w, tw, tc, bank, M=128):
        s = self.s
        for kc in range(NKC):
            s.op("pe", lambda e, kc=kc: e.matmul(self.B[bank][0:M, :], lhsT=w[:, kc, 0:M],
                                                 rhs=self.hT[:, kc, tc * 512:(tc + 1) * 512],
                                                 start=(kc == 0), stop=(kc == NKC - 1)),
                 R=[tw, self.t_hT], W=[self.t_B[bank]])

    def proj_plain(self, wblk_ap, name, dst, tdst, silu=False, M=128, eng_alt=True):
        s = self.s
        w, tw = self.load_wblk(wblk_ap, name)
        for tc in range(4):
            b = self.next_bank()
            self.proj_chunk(w, tw, tc, b, M)
            d = dst[0:M, tc * 512:(tc + 1) * 512]
            if silu:
                s.op("act", lambda e: e.activation(out=d, in_=self.B[b][0:M, :], func=AF.Silu),
                     R=[self.t_B[b]], W=[tdst])
            elif eng_alt and tc % 2 == 1:
                s.op("dve", lambda e: e.tensor_copy(out=d, in_=self.B[b][0:M, :]), R=[self.t_B[b]], W=[tdst])
            else:
                s.op("act", lambda e: e.copy(out=d, in_=self.B[b][0:M, :]), R=[self.t_B[b]], W=[tdst])

    def proj_rope(self, wblk_ap, name, sname, dst, tdst, rope_ap, tbl):
        s = self.s
        w, tw = self.load_wblk(wblk_ap, name)
        ws, tws = self.load_wblk(wblk_ap, sname)
        for tc in range(4):
            r = tc % 2
            s.dma("sp", self.rt[r][:, 0, :], rope_ap[2 * tbl, :, tc * 512:(tc + 1) * 512], self.t_rt[r], W=[self.t_rt[r]])
            s.dma("sp", self.rt[r][:, 1, :], rope_ap[2 * tbl + 1, :, tc * 512:(tc + 1) * 512], self.t_rt[r], W=[self.t_rt[r]])
            b0 = self.next_bank()
            self.proj_chunk(w, tw, tc, b0)
            b1 = self.next_bank()
            self.proj_chunk(ws, tws, tc, b1)
            f0, f1 = self.ft[2 * r], self.ft[2 * r + 1]
            t0, t1 = self.t_ft[2 * r], self.t_ft[2 * r + 1]
            s.op("dve", lambda e: e.tensor_tensor(out=f0[:], in0=self.B[b0][:], in1=self.rt[r][:, 0, :], op=ALU.mult),
                 R=[self.t_B[b0], self.t_rt[r]], W=[t0])
            s.op("dve", lambda e: e.tensor_tensor(out=f1[:], in0=self.B[b1][:], in1=self.rt[r][:, 1, :], op=ALU.mult),
                 R=[self.t_B[b1], self.t_rt[r]], W=[t1])
            d = dst[:, tc * 512:(tc + 1) * 512]
            s.op("dve", lambda e: e.tensor_tensor(out=d, in0=f0[:], in1=f1[:], op=ALU.add), R=[t0, t1], W=[tdst])

    def proj_v(self, wv_ap, grp):
        s = self.s
        s.dma("pool", self.wvb[:], wv_ap[grp], self.t_wvb, W=[self.t_wvb])
        for tb in range(NTB):
            b = self.next_bank()
            for kc in range(NKC):
                s.op("pe", lambda e, kc=kc: e.matmul(self.B[b][:, 0:256], lhsT=self.hT[:, kc, tb * 128:(tb + 1) * 128],
                                                     rhs=self.wvb[:, kc, :], start=(kc == 0), stop=(kc == NKC - 1)),
                     R=[self.t_wvb, self.t_hT], W=[self.t_B[b]])
            if tb % 2 == 0:
                s.op("act", lambda e: e.copy(out=self.v[:, tb, :], in_=self.B[b][:, 0:256]), R=[self.t_B[b]], W=[self.t_v])
            else:
                s.op("dve", lambda e: e.tensor_copy(out=self.v[:, tb, :], in_=self.B[b][:, 0:256]), R=[self.t_B[b]], W=[self.t_v])

    def attention(self, units):
        s = self.s
        steps = []
        for u, (job, qc) in enumerate(units):
            nkb = 4 * qc + 4
            pair = (0, 1) if u % 2 == 0 else (2, 3)
            for kb in range(nkb):
                steps.append((job, qc, kb, kb == 0, kb == nkb - 1, pair))
        sbank = [5, 6, 4]

        def emit_qk(i):
            job, qc, kb, first, last, pair = steps[i]
            sb_ = sbank[i % 3]
            lo = max(0, kb - 4 * qc) * 128
            extra = [(jj, lhsT, self.ident[:], tk) for (jj, lhsT, tk) in (job["masks"](kb, qc) if job.get("masks") else [])]
            if kb >= 4 * qc:
                extra.append((kb - 4 * qc, self.ident[:], self.negtri[:], self.t_const))
            s.op("pe", lambda e: e.matmul(self.B[sb_][:, lo:512], lhsT=job["k"](kb), rhs=job["q"](qc * 512 + lo, (qc + 1) * 512),
                                          start=True, stop=(len(extra) == 0)),
                 R=job["R"], W=[self.t_B[sb_]])
            for n, (jj, lhsT, rhs, tk) in enumerate(extra):
                s.op("pe", lambda e: e.matmul(self.B[sb_][:, jj * 128:(jj + 1) * 128], lhsT=lhsT, rhs=rhs,
                                              start=False, stop=(n == len(extra) - 1)),
                     R=[tk, self.t_const], W=[self.t_B[sb_]])

        def emit_rest(i):
            job, qc, kb, first, last, pair = steps[i]
            sb_ = sbank[i % 3]
            lo = max(0, kb - 4 * qc) * 128
            p = i % self.NPT
            pt, tpt = self.pt[p], self.t_pt[p]
            bias = job["bias"](kb, qc) if job.get("bias") else None
            if bias is not None:
                s.op("act", lambda e: e.activation(out=pt[:, lo:512], in_=self.B[sb_][:, lo:512], func=AF.Exp, bias=bias, scale=job["scale"]),
                     R=[self.t_B[sb_]] + job["Rb"], W=[tpt])
            else:
                s.op("act", lambda e: e.activation(out=pt[:, lo:512], in_=self.B[sb_][:, lo:512], func=AF.Exp, scale=job["scale"]),
                     R=[self.t_B[sb_]], W=[tpt])
            ob, db = pair
            s.op("pe", lambda e: e.matmul(self.B[ob][:, lo:512], lhsT=job["v"](kb), rhs=pt[:, lo:512], start=first, stop=last),
                 R=[tpt] + job["Rv"], W=[self.t_B[ob]])
            s.op("pe", lambda e: e.matmul(self.B[db][:, lo:512], lhsT=self.onesb[:], rhs=pt[:, lo:512], start=first, stop=last),
                 R=[tpt, self.t_const], W=[self.t_B[db]])
            if last:
                job["epi"](qc, pair)

        n = len(steps)
        LA = 2
        for i in range(min(LA, n)):
            emit_qk(i)
        for i in range(n):
            if i + LA < n:
                emit_qk(i + LA)
            emit_rest(i)

    def mkjob(self, j, epi, rows=(0, 128), scale=128 ** -0.5, biasT=None, masks=None):
        lo_p, hi_p = rows
        d = dict(
            q=lambda lo, hi: self.qT[lo_p:hi_p, j, lo:hi],
            k=lambda kb: self.kT[lo_p:hi_p, j, kb * 128:(kb + 1) * 128],
            v=lambda kb: self.v[:, kb, j * 128:(j + 1) * 128],
            R=[self.t_qT, self.t_kT], Rv=[self.t_v], Rb=[self.t_sm2],
            scale=scale, epi=epi, masks=masks)
        if biasT is not None:
            d["bias"] = lambda kb, qc: biasT[:, j, qc, kb:kb + 1]
        return d

    def od_normalize(self, pair, fr, tr, fo, to):
        s = self.s
        ob, db = pair
        s.op("dve", lambda e: e.reciprocal(out=fr[:], in_=self.B[db][:]), R=[self.t_B[db]], W=[tr])
        s.op("dve", lambda e: e.tensor_tensor(out=fo[:], in0=self.B[ob][:], in1=fr[:], op=ALU.mult), R=[self.t_B[ob], tr], W=[to])

    def normalize_gate(self, src, tsrc, j, qc, gain, dst_ch, nb=4, scr=(0, 1)):
        s = self.s
        c0, c1 = qc * 512, (qc + 1) * 512
        s.op("act", lambda e: e.activation(out=self.sq[:], in_=src[:], func=AF.Square), R=[tsrc], W=[self.t_sq])
        s.op("pe", lambda e: e.matmul(self.B[nb][:], lhsT=self.onesb[:], rhs=self.sq[:], start=True, stop=True),
             R=[self.t_sq, self.t_const], W=[self.t_B[nb]])
        f4, t4 = self.ft[scr[0]], self.t_ft[scr[0]]
        s.op("act", lambda e: e.activation(out=f4[:], in_=self.B[nb][:], func=AF.Sqrt, bias=self.epsc[:, 0:1], scale=1.0 / 128),
             R=[self.t_B[nb], self.t_const], W=[t4])
        s.op("dve", lambda e: e.reciprocal(out=f4[:], in_=f4[:]), R=[t4], W=[t4])
        f5, t5 = self.ft[scr[1]], self.t_ft[scr[1]]
        s.op("dve", lambda e: e.tensor_tensor(out=f5[:], in0=src[:], in1=f4[:], op=ALU.mult), R=[tsrc, t4], W=[t5])
        s.op("dve", lambda e: e.scalar_tensor_tensor(out=self.mixT[:, dst_ch, c0:c1], in0=f5[:], scalar=gain, in1=self.sg[:, j, c0:c1],
                                                     op0=ALU.mult, op1=ALU.mult),
             R=[t5, self.t_sg, self.t_par, self.t_sm], W=[self.t_mix[dst_ch]])

    def fox(self, W):
        s = self.s
        wblk, wv, wf = W["wblk"], W["wv"], W["wf"]
        for j in range(2):
            self.proj_plain(wblk, f"fq{j}", self.qT[:, j, :], self.t_qT)
            self.proj_plain(wblk, f"fk{j}", self.kT[:, j, :], self.t_kT)
            self.proj_plain(wblk, f"fg{j}", self.sg[:, j, :], self.t_sg, silu=True)
        self.proj_v(wv, 0)
        s.dma("pool", self.wfb[:], wf, self.t_wfb, W=[self.t_wfb])
        b = 0
        for tb in range(NTB):
            for kc in range(NKC):
                s.op("pe", lambda e: e.matmul(self.B[b][:, tb * 2:(tb + 1) * 2], lhsT=self.hT[:, kc, tb * 128:(tb + 1) * 128],
                                              rhs=self.wfb[:, kc, :], start=(kc == 0), stop=(kc == NKC - 1)),
                     R=[self.t_wfb, self.t_hT], W=[self.t_B[b]])
        m2, t2 = self.sm2, self.t_sm2
        zb = m2[:, 0:32]
        z3 = zb.rearrange("p (t h) -> p t h", h=2)
        for h in range(2):
            s.op("dve", lambda e: e.tensor_scalar(out=z3[:, :, h], in0=self.B[b][:, 0:32].rearrange("p (t h) -> p t h", h=2)[:, :, h],
                                                  scalar1=self.fb[:, h:h + 1], scalar2=None, op0=ALU.add),
                 R=[self.t_B[b], self.t_par], W=[t2])
        s.op("act", lambda e: e.activation(out=zb, in_=zb, func=AF.Exp, scale=-1.0), R=[t2], W=[t2])
        s.op("act", lambda e: e.activation(out=zb, in_=zb, func=AF.Ln, bias=self.epsc[:, 1:2]), R=[t2, self.t_const], W=[t2])
        s.op("dve", lambda e: e.tensor_scalar(out=zb, in0=zb, scalar1=-1.0, scalar2=None, op0=ALU.mult), R=[t2], W=[t2])
        s.op("pe", lambda e: e.matmul(self.B[1][:, 0:32], lhsT=self.cf32[:, 128:256], rhs=zb, start=True, stop=True),
             R=[t2, self.t_const], W=[self.t_B[1]])
        s.op("pe", lambda e: e.matmul(self.B[2][:, 0:32], lhsT=self.cf32[:, 256:384], rhs=zb, start=True, stop=True),
             R=[t2, self.t_const], W=[self.t_B[2]])
        tot = m2[:, 32:64]
        incl = m2[:, 64:96]
        Ft = m2[:, 96:128]
        tot3 = tot.rearrange("p (t h) -> p t h", h=2)
        I3 = incl.rearrange("p (t h) -> p t h", h=2)
        F3 = Ft.rearrange("p (t h) -> p t h", h=2)
        s.op("dve", lambda e: e.tensor_copy(out=tot, in_=self.B[2][:, 0:32]), R=[self.t_B[2]], W=[t2])
        for h in range(2):
            s.op("dve", lambda e: e.tensor_tensor_scan(out=I3[:, :, h], data0=self.cf32[:, 256:256 + NTB], data1=tot3[:, :, h],
                                                       initial=0.0, op0=ALU.mult, op1=ALU.add),
                 R=[t2, self.t_const], W=[t2])
        s.op("dve", lambda e: e.tensor_tensor(out=Ft, in0=incl, in1=tot, op=ALU.subtract), R=[t2], W=[t2])
        s.op("dve", lambda e: e.tensor_tensor(out=Ft, in0=Ft, in1=self.B[1][:, 0:32], op=ALU.add), R=[t2, self.t_B[1]], W=[t2])
        biasT = m2[:, 128:256].rearrange("p (h q k) -> p h q k", h=2, q=4)
        for h in range(2):
            for qc in range(4):
                s.op("dve", lambda e: e.tensor_scalar(out=biasT[:, h, qc, :], in0=F3[:, :, h], scalar1=I3[:, 4 * qc + 3, h:h + 1],
                                                      scalar2=-1.0, op0=ALU.subtract, op1=ALU.mult),
                     R=[t2], W=[t2])
        units = []
        for j in range(2):
            def epi(qc, pair, j=j):
                c0, c1 = qc * 512, (qc + 1) * 512
                self.od_normalize(pair, self.ft[0], self.t_ft[0], self.ft[1], self.t_ft[1])
                s.op("pool", lambda e: e.tensor_tensor(out=self.mixT[:, j, c0:c1], in0=self.ft[1][:], in1=self.sg[:, j, c0:c1], op=ALU.mult),
                     R=[self.t_ft[1], self.t_sg], W=[self.t_mix[j]])
            job = self.mkjob(j, epi, biasT=biasT)
            units += [(job, qc) for qc in range(4)]
        self.attention(units)

    def diff(self, W, rope_ap):
        s = self.s
        wblk, wv = W["wblk"], W["wv"]
        for j in range(2):
            self.proj_rope(wblk, f"dq{j}", f"dqs{j}", self.qT[:, j, :], self.t_qT, rope_ap, 0)
            self.proj_rope(wblk, f"dk{j}", f"dks{j}", self.kT[:, j, :], self.t_kT, rope_ap, 0)
            self.proj_plain(wblk, f"dg{j}", self.sg[:, j, :], self.t_sg, silu=True)
        self.proj_v(wv, 1)
        sm, tsm = self.sm, self.t_sm
        l4 = self.lam[:, :].rearrange("p (a b c) -> p a b c", a=2, b=2)
        prod = self.sm3[:, 0:128].rearrange("p (a c) -> p a c", a=2)
        s.op("dve", lambda e: e.tensor_tensor(out=prod, in0=l4[:, :, 0, :], in1=l4[:, :, 1, :], op=ALU.mult), R=[self.t_par], W=[self.t_sm3])
        s.op("dve", lambda e: e.tensor_reduce(out=sm[:, 8:10], in_=prod, axis=AX.X, op=ALU.add), R=[self.t_sm3], W=[tsm])
        s.op("act", lambda e: e.activation(out=sm[:, 10:12], in_=sm[:, 8:10], func=AF.Exp), R=[tsm], W=[tsm])
        s.op("dve", lambda e: e.tensor_tensor(out=sm[:, 12:13], in0=sm[:, 11:12], in1=sm[:, 10:11], op=ALU.subtract), R=[tsm], W=[tsm])
        s.op("dve", lambda e: e.tensor_tensor(out=sm[:, 12:13], in0=sm[:, 12:13], in1=self.lsc[:, 0:1], op=ALU.subtract), R=[tsm, self.t_par], W=[tsm])
        s.op("dve", lambda e: e.tensor_tensor(out=sm[:, 13:14], in0=self.dng[:, 0:1], in1=self.lsc[:, 1:2], op=ALU.mult), R=[tsm, self.t_par], W=[tsm])
        units = []
        for j in range(2):
            def epi1(qc, pair, j=j):
                self.od_normalize(pair, self.ft[0], self.t_ft[0], self.ft[1], self.t_ft[1])

            def epi2(qc, pair, j=j):
                self.od_normalize(pair, self.ft[0], self.t_ft[0], self.ft[2], self.t_ft[2])
                f3, t3 = self.ft[3], self.t_ft[3]
                s.op("dve", lambda e: e.scalar_tensor_tensor(out=f3[:], in0=self.ft[2][:], scalar=sm[:, 12:13], in1=self.ft[1][:],
                                                             op0=ALU.mult, op1=ALU.add),
                     R=[self.t_ft[2], self.t_ft[1], tsm], W=[t3])
                self.normalize_gate(f3, t3, j, qc, sm[:, 13:14], 2 + j, nb=pair[1], scr=(0, 2))
            j1 = self.mkjob(j, epi1, rows=(0, 64), scale=64 ** -0.5)
            j2 = self.mkjob(j, epi2, rows=(64, 128), scale=64 ** -0.5)
            for qc in range(4):
                units += [(j1, qc), (j2, qc)]
        self.attention(units)

    def moba(self, W, rope_ap):
        s = self.s
        wblk, wv = W["wblk"], W["wv"]
        for j in range(2):
            self.proj_rope(wblk, f"mq{j}", f"mqs{j}", self.qT[:, j, :], self.t_qT, rope_ap, 1)
            self.proj_rope(wblk, f"mk{j}", f"mks{j}", self.kT[:, j, :], self.t_kT, rope_ap, 1)
            self.proj_plain(wblk, f"mg{j}", self.sg[:, j, :], self.t_sg, silu=True)
        self.proj_v(wv, 2)
        m2, t2 = self.sm2, self.t_sm2
        m3, t3 = self.sm3, self.t_sm3
        for j in range(2):
            s.op("dve", lambda e: e.tensor_reduce(out=m3[:, 0:8], in_=self.kT[:, j, :].rearrange("p (n l) -> p n l", l=256), axis=AX.X, op=ALU.add),
                 R=[self.t_kT], W=[t3])
            s.op("dve", lambda e: e.tensor_scalar(out=self.kmT[:, :], in0=m3[:, 0:8], scalar1=1.0 / 256, scalar2=None, op0=ALU.mult),
                 R=[t3], W=[self.t_kmT])
            gb = 4
            for qb in range(8, 16):
                s.op("pe", lambda e: e.matmul(self.B[gb][:, (qb - 8) * 8:(qb - 7) * 8], lhsT=self.qT[:, j, qb * 128:(qb + 1) * 128],
                                              rhs=self.kmT[:, :], start=True, stop=True),
                     R=[self.t_qT, self.t_kmT], W=[self.t_B[gb]])
            gsb = m2[:, 0:64].rearrange("p (s n) -> p s n", n=8)
            gps = self.B[gb][:, 0:64].rearrange("p (s n) -> p s n", n=8)
            s.op("dve", lambda e: e.memset(m2[:, 0:64], -1e30), W=[t2])
            for own in range(4, 8):
                sl = slice(2 * own - 8, 2 * own - 6)
                s.op("dve", lambda e: e.tensor_copy(out=gsb[:, sl, 0:own], in_=gps[:, sl, 0:own]), R=[self.t_B[gb]], W=[t2])
            top = m2[:, 64:128].rearrange("p (s n) -> p s n", n=8)
            for sl in range(8):
                s.op("dve", lambda e: e.max(out=top[:, sl, :], in_=gsb[:, sl, :]), R=[t2], W=[t2])
            for sl in range(8):
                s.op("dve", lambda e: e.tensor_scalar(out=self.nsel[:, sl, :], in0=gsb[:, sl, :], scalar1=top[:, sl, 2:3], scalar2=NEG,
                                                      op0=ALU.is_lt, op1=ALU.mult),
                     R=[t2], W=[self.t_nsel])
            for sl in range(8):
                s.op("dve", lambda e: e.tensor_copy(out=self.negB[:, sl, :, :],
                                                     in_=self.nsel[:, sl, :].rearrange("p (n o) -> p n o", o=1).to_broadcast([128, 8, 128])),
                     R=[self.t_nsel], W=[self.t_scr])

            def masks(kb, qc):
                out = []
                j0 = max(0, kb - 4 * qc)
                for jj in range(j0, 4):
                    qb = 4 * qc + jj
                    own = qb // 2
                    if own >= 4 and kb // 2 < own:
                        out.append((jj, self.negB[:, qb - 8, kb // 2, :], self.t_scr))
                return out

            def epi(qc, pair, j=j):
                c0, c1 = qc * 512, (qc + 1) * 512
                self.od_normalize(pair, self.ft[0], self.t_ft[0], self.ft[1], self.t_ft[1])
                s.op("pool", lambda e: e.tensor_tensor(out=self.mixT[:, 4 + j, c0:c1], in0=self.ft[1][:], in1=self.sg[:, j, c0:c1], op=ALU.mult),
                     R=[self.t_ft[1], self.t_sg], W=[self.t_mix[4 + j]])
            job = self.mkjob(j, epi, masks=masks)
            self.attention([(job, qc) for qc in range(4)])

    def gla(self, W):
        s = self.s
        wblk, wv = W["wblk"], W["wv"]
        sm, tsm = self.sm, self.t_sm
        m3, t3 = self.sm3, self.t_sm3
        tscr = self.t_scr
        qg, kend, kgz = self.qT[:, 0, :], self.qT[:, 1, :], self.kT
        self.proj_plain(wblk, "ga", self.kT[:, 1, :], self.t_aT, M=16, eng_alt=False)
        s.op("dve", lambda e: e.tensor_scalar(out=sm[:, 14:15], in0=self.gba[:, 0:1], scalar1=-1.0, scalar2=None, op0=ALU.mult),
             R=[self.t_par], W=[tsm])
        for tc in range(4):
            b = self.next_bank()
            c = slice(tc * 512, (tc + 1) * 512)
            s.op("pe", lambda e: e.matmul(self.B[b][:], lhsT=self.wa2[0:16, :], rhs=self.aT[0:16, c], start=True, stop=True),
                 R=[self.t_wa2, self.t_aT], W=[self.t_B[b]])
            s.op("act", lambda e: e.activation(out=self.gcs[:, c], in_=self.B[b][:], func=AF.Exp, bias=sm[:, 14:15], scale=-1.0),
                 R=[self.t_B[b], tsm], W=[tscr])
        s.op("act", lambda e: e.activation(out=self.gcs[:, :], in_=self.gcs[:, :], func=AF.Ln, bias=self.epsc[:, 1:2]), R=[tscr, self.t_const], W=[tscr])
        s.op("dve", lambda e: e.tensor_scalar(out=self.gcs[:, :], in0=self.gcs[:, :], scalar1=-1.0 / 16.0, scalar2=None, op0=ALU.mult),
             R=[tscr], W=[tscr])
        for c in range(NTB):
            cs = slice(c * 128, (c + 1) * 128)
            s.op("dve", lambda e: e.tensor_tensor_scan(out=self.gcs[:, cs], data0=self.cf32[:, 256:384], data1=self.gcs[:, cs],
                                                       initial=0.0, op0=ALU.mult, op1=ALU.add),
                 R=[tscr, self.t_const], W=[tscr])
        s.op("act", lambda e: e.activation(out=self.eg[:, :], in_=self.gcs[:, :], func=AF.Exp), R=[tscr], W=[tscr])
        s.op("act", lambda e: e.activation(out=self.eng[:, :], in_=self.gcs[:, :], func=AF.Exp, scale=-1.0), R=[tscr], W=[tscr])
        egl = m3[:, 0:16]
        s.op("act", lambda e: e.activation(out=egl, in_=self.gcs[:, :].rearrange("p (c t) -> p c t", t=128)[:, :, 127], func=AF.Exp),
             R=[tscr], W=[t3])
        s.op("dve", lambda e: e.tensor_tensor(out=self.eend[:, :].rearrange("p (c t) -> p c t", t=128),
                                              in0=self.eng[:, :].rearrange("p (c t) -> p c t", t=128),
                                              in1=egl.rearrange("p (c o) -> p c o", o=1).to_broadcast([128, NTB, 128]), op=ALU.mult),
             R=[tscr, t3], W=[tscr])
        w, tw = self.load_wblk(wblk, "gq")
        for tc in range(4):
            b = self.next_bank()
            c = slice(tc * 512, (tc + 1) * 512)
            self.proj_chunk(w, tw, tc, b)
            s.op("dve", lambda e: e.scalar_tensor_tensor(out=qg[:, c], in0=self.B[b][:], scalar=0.125, in1=self.eg[:, c], op0=ALU.mult, op1=ALU.mult),
                 R=[self.t_B[b], tscr], W=[self.t_qT])
        w, tw = self.load_wblk(wblk, "gk")
        s.op("dve", lambda e: e.memset(kgz[:, :, :], 0.0), W=[self.t_kT])
        for tc in range(4):
            b = self.next_bank()
            c = slice(tc * 512, (tc + 1) * 512)
            self.proj_chunk(w, tw, tc, b)
            for j in range(2):
                r = slice(j * 64, (j + 1) * 64)
                s.op("dve", lambda e: e.tensor_tensor(out=kgz[r, j, c], in0=self.B[b][r, :], in1=self.eng[r, c], op=ALU.mult),
                     R=[self.t_B[b], tscr], W=[self.t_kT])
            s.op("dve", lambda e: e.tensor_tensor(out=kend[:, c], in0=self.B[b][:], in1=self.eend[:, c], op=ALU.mult), R=[self.t_B[b], tscr], W=[self.t_qT])
        for half in range(2):
            for k in range(8):
                c = half * 8 + k
                s.op("pe", lambda e: e.transpose(out=self.T[:, k * 128:(k + 1) * 128], in_=kend[:, c * 128:(c + 1) * 128], identity=self.ident[:]),
                     R=[self.t_qT, self.t_const], W=[self.t_T])
            s.op("act", lambda e: e.copy(out=self.ktok[:, half * 8:(half + 1) * 8, :], in_=self.T[:, :].rearrange("p (k t) -> p k t", t=128)),
                 R=[self.t_T], W=[self.t_ktok])
        for j in range(2):
            self.proj_plain(wblk, f"gg{j}", self.sg[:, j, :], self.t_sg, silu=True)
        self.proj_v(wv, 3)
        if getattr(self, "pre_b", None) is not None:
            self.pre_b()
        s.op("dve", lambda e: e.memset(self.S32[:], 0.0), W=[self.t_S32])
        s.op("dve", lambda e: e.memset(self.S16[:, :, :], 0.0), W=[self.t_S16])
        tri2 = self.tri[:, :].rearrange("p (o t) -> p o t", o=1).to_broadcast([128, 2, 128])
        ats, t_ats = [self.at, self.at2], [self.t_at, self.t_at2]
        S16s, t_S16s = [self.S16, self.S16b], [self.t_S16, self.t_S16b]
        s.op("dve", lambda e: e.memset(self.S16b[:, :, :], 0.0), W=[self.t_S16b])
        ubs = (6, 4)
        ab = 5

        def emit_AT(c):
            cs = slice(c * 128, (c + 1) * 128)
            for j in range(2):
                s.op("pe", lambda e: e.matmul(self.B[ab][:, j * 128:(j + 1) * 128], lhsT=kgz[:, j, cs], rhs=qg[:, cs], start=True, stop=True),
                     R=[self.t_qT, self.t_kT], W=[self.t_B[ab]])
            s.op("dve", lambda e: e.tensor_tensor(out=ats[c % 2][:, :, :], in0=self.B[ab][:, 0:256].rearrange("p (j t) -> p j t", j=2), in1=tri2, op=ALU.mult),
                 R=[self.t_B[ab], self.t_const], W=[t_ats[c % 2]])

        def emit_U(c):
            ub = ubs[c % 2]
            s.op("pe", lambda e: e.matmul(self.B[ub][:, 0:256], lhsT=self.ktok[:, c, :], rhs=self.v[:, c, :], start=True, stop=True),
                 R=[self.t_ktok, self.t_v], W=[self.t_B[ub]])

        def emit_o(c, obs, cc):
            cs = slice(c * 128, (c + 1) * 128)
            for j in range(2):
                ob = obs[j]
                s.op("pe", lambda e: e.matmul(self.B[ob][:, cc * 128:(cc + 1) * 128], lhsT=self.v[:, c, j * 128:(j + 1) * 128], rhs=ats[c % 2][:, j, :],
                                              start=True, stop=False),
                     R=[self.t_v, t_ats[c % 2]], W=[self.t_B[ob]])
                s.op("pe", lambda e: e.matmul(self.B[ob][:, cc * 128:(cc + 1) * 128], lhsT=S16s[c % 2][:, j, :], rhs=qg[:, cs], start=False, stop=True),
                     R=[t_S16s[c % 2], self.t_qT], W=[self.t_B[ob]])

        def emit_upd(c):
            ub = ubs[c % 2]
            for j in range(2):
                r = slice(j * 64, (j + 1) * 64)
                s.op("dve", lambda e: e.scalar_tensor_tensor(out=self.S32[r, :], in0=self.S32[r, :], scalar=egl[r, c:c + 1],
                                                             in1=self.B[ub][r, j * 128:(j + 1) * 128], op0=ALU.mult, op1=ALU.add),
                     R=[self.t_S32, t3, self.t_B[ub]], W=[self.t_S32])
            nxt = (c + 1) % 2
            for j in range(2):
                r = slice(j * 64, (j + 1) * 64)
                s.op("act", lambda e: e.copy(out=S16s[nxt][r, j, :], in_=self.S32[r, :]), R=[self.t_S32], W=[t_S16s[nxt]])

        emit_AT(0)
        emit_U(0)
        for tg in range(4):
            obs = (0, 1) if tg % 2 == 0 else (2, 3)
            for cc in range(4):
                c = tg * 4 + cc
                if c + 1 < NTB:
                    emit_AT(c + 1)
                    emit_U(c + 1)
                emit_o(c, obs, cc)
                if c + 1 < NTB:
                    emit_upd(c)
            for j in range(2):
                f3, tf3 = self.ft[3], self.t_ft[3]
                s.op("act", lambda e: e.copy(out=f3[:], in_=self.B[obs[j]][:]), R=[self.t_B[obs[j]]], W=[tf3])
                self.normalize_gate(f3, tf3, j, tg, self.gng[:, 0:1], 6 + j, nb=obs[j], scr=(0, 1))

    def program_a(self, x_ap, W, rope_ap, mix_out_ap, groups=("fox", "diff", "moba", "gla"), t_x=None, mixd=None, t_mixd=None):
        s = self.s
        self.load_params(W)
        self.norm_phase(x_ap, W["gbc"], t_x)
        hook = mixd if callable(mixd) else (lambda g: None)
        if "fox" in groups:
            self.fox(W)
            hook(0)
        if "diff" in groups:
            self.diff(W, rope_ap)
            hook(1)
        if "moba" in groups:
            self.moba(W, rope_ap)
            hook(2)
        if "gla" in groups:
            self.gla(W)
            hook(3)
        if callable(mixd):
            return
        for ch in range(8):
            g, j = ch // 2, ch % 2
            if ("fox", "diff", "moba", "gla")[g] not in groups:
                continue
            if mixd is not None:
                s.dma("sp", mixd[ch * 128:(ch + 1) * 128, :], self.mixT[:, ch, :], self.t_mix[ch], R=[self.t_mix[ch]], W=[t_mixd])
                continue
            s.dma("pool", mix_out_ap[ch], self.mixT[:, ch, :], self.t_mix[ch], R=[self.t_mix[ch]], W=[self.t_out])


def _dram(nc, name, shape, kind="ExternalInput", dt=F32):
    return nc.dram_tensor(name, list(shape), dt, kind=kind).ap()


A_INPUTS = (("x", [S, D]), ("wblk", [NBLK, 128, NKC, 128]), ("wv", [4, 128, NKC, 256]), ("wf", [128, NKC, 2]),
            ("gbc", [128, D]), ("fb", [128, 2]), ("lam", [128, 256]), ("lsc", [128, 8]), ("dng", [128, 1]),
            ("gng", [128, 1]), ("gba", [128, 1]), ("wa2", [16, 128]), ("rope", [4, 128, S]), ("consts", [128, 384]))


def build_a(groups=("fox", "diff", "moba", "gla")):
    nc = bass.Bass("TRN2", target_bir_lowering=False)
    aps = {n: _dram(nc, n, shp) for n, shp in A_INPUTS}
    mix = _dram(nc, "mix", [8, 128, S], kind="ExternalOutput")
    p = Prog(nc)
    p.load_consts(aps["consts"])
    p.program_a(aps["x"], aps, aps["rope"], mix, groups)
    p.s.finish([p.t_out])
    return nc, p


def layer_inputs_a(inputs, l, b, hh, rope, consts):
    wblk, wv, wf = pack_layer_weights(inputs["w_in"][l], hh)
    lambda_init = 0.8 - 0.6 * math.exp(-0.3 * l)
    lsc = np.zeros((128, 8), np.float32)
    lsc[:, 0] = lambda_init
    lsc[:, 1] = 1.0 - lambda_init
    bc = lambda v, n: np.ascontiguousarray(np.broadcast_to(np.asarray(v, np.float32).reshape(1, n), (128, n)))
    return {
        "x": np.ascontiguousarray(inputs["x"][b]),
        "wblk": wblk, "wv": wv, "wf": wf,
        "gbc": bc(inputs["norm_g"][l], D),
        "fb": bc(inputs["fox_fb"][l][2 * hh:2 * hh + 2], 2),
        "lam": bc(inputs["diff_lam"][l].reshape(-1), 256),
        "lsc": lsc,
        "dng": np.ascontiguousarray(inputs["diff_norm_g"][l].reshape(128, 1)),
        "gng": np.ascontiguousarray(inputs["gla_norm_g"][l].reshape(128, 1)),
        "gba": np.ascontiguousarray(inputs["gla_ba"][l][hh * 128:(hh + 1) * 128].reshape(128, 1)),
        "wa2": np.ascontiguousarray(inputs["gla_wa2"][l][:, hh * 128:(hh + 1) * 128]),
        "rope": rope, "consts": consts,
    }


def emit_b(p, x_ap, mixf_ap, wout_ap, T, xo_ap=None, fg_ap=None, y_ap=None):
    s, nc = p.s, p.nc
    wo = nc.alloc_sbuf_tensor_at("wo", [128, 16, D], BF16, offset=p.hT_off)
    mf = nc.alloc_sbuf_tensor_at("mf", [128, 16, T], BF16, offset=p.mix_off)
    yt = nc.alloc_sbuf_tensor_at("yt", [128, D], F32, offset=p.g0 + 16384)
    t_wo = s.toks("wo", 16)
    t_mf = s.toks("mf", 16)
    for c in range(16):
        s.dma("pool", wo[:, c, :], wout_ap[c], t_wo[c], R=[], W=[t_wo[c], p.t_hT])
        s.dma("pool", mf[:, c, :], mixf_ap[c], t_mf[c], R=[], W=[t_mf[c]] + ([p.t_mix[c // 2]] if False else []))
    if fg_ap is not None:
        s.dma("sp", p.gbc[:], fg_ap, p.t_gbc, W=[p.t_gbc])
    for tb in range(T // 128):
        i = tb % 2
        xt, txt = p.xt[i], p.t_xt[i]
        s.dma("sp", xt[:], x_ap[tb * 128:(tb + 1) * 128, :], txt, W=[txt])
        for fc in range(4):
            b = p.next_bank()
            for c in range(16):
                s.op("pe", lambda e: e.matmul(p.B[b][:], lhsT=mf[:, c, tb * 128:(tb + 1) * 128], rhs=wo[:, c, fc * 512:(fc + 1) * 512],
                                              start=(c == 0), stop=(c == 15)),
                     R=[t_wo[c], t_mf[c]], W=[p.t_B[b]])
            s.op("dve", lambda e: e.tensor_tensor(out=xt[:, fc * 512:(fc + 1) * 512], in0=xt[:, fc * 512:(fc + 1) * 512], in1=p.B[b][:], op=ALU.add),
                 R=[p.t_B[b], txt], W=[txt])
        if xo_ap is not None:
            s.dma("sp", xo_ap[tb * 128:(tb + 1) * 128, :], xt[:], txt, R=[txt], W=[p.t_out])
        if y_ap is not None:
            ss = p.sm[:, 0:1]
            s.op("act", lambda e: e.activation(out=yt[:], in_=xt[:], func=AF.Square, accum_out=ss), R=[txt], W=[p.t_v, p.t_sm])
            s.op("act", lambda e: e.activation(out=p.sm[:, 1:2], in_=ss, func=AF.Sqrt, bias=p.epsc[:, 0:1], scale=1.0 / D),
                 R=[p.t_sm, p.t_const], W=[p.t_sm])
            s.op("dve", lambda e: e.reciprocal(out=p.sm[:, 2:3], in_=p.sm[:, 1:2]), R=[p.t_sm], W=[p.t_sm])
            s.op("dve", lambda e: e.scalar_tensor_tensor(out=yt[:], in0=xt[:], scalar=p.sm[:, 2:3], in1=p.gbc[:], op0=ALU.mult, op1=ALU.mult),
                 R=[txt, p.t_sm, p.t_gbc], W=[p.t_v])
            s.dma("sp", y_ap[tb * 128:(tb + 1) * 128, :], yt[:], p.t_v, R=[p.t_v], W=[p.t_out])


def build_b(T, final):
    nc = bass.Bass("TRN2", target_bir_lowering=False)
    x = _dram(nc, "x", [T, D])
    mixf = _dram(nc, "mixf", [16, 128, T])
    wout = _dram(nc, "wout", [16, 128, D])
    consts = _dram(nc, "consts", [128, 384])
    p = Prog(nc)
    p.load_consts(consts)
    if final:
        fg = _dram(nc, "fg", [128, D])
        y = _dram(nc, "y", [T, D], kind="ExternalOutput")
        emit_b(p, x, mixf, wout, T, fg_ap=fg, y_ap=y)
    else:
        xo = _dram(nc, "xo", [T, D], kind="ExternalOutput")
        emit_b(p, x, mixf, wout, T, xo_ap=xo)
    p.s.finish([p.t_out])
    return nc, p


def wout_perm(w_out_l):
    rows = []
    for r in range(2):
        for g in range(4):
            for j in range(2):
                rows.append(g * 512 + (2 * r + j) * 128 + np.arange(128))
    return np.ascontiguousarray(w_out_l[np.concatenate(rows)].reshape(16, 128, D))


_CACHE = {}


def kernel(x, norm_g, w_in, fox_fb, diff_lam, diff_norm_g, gla_wa2, gla_ba, gla_norm_g, w_out, final_norm_g):
    inputs = dict(x=np.asarray(x, np.float32), norm_g=np.asarray(norm_g, np.float32), w_in=np.asarray(w_in, np.float32),
                  fox_fb=np.asarray(fox_fb, np.float32), diff_lam=np.asarray(diff_lam, np.float32),
                  diff_norm_g=np.asarray(diff_norm_g, np.float32), gla_wa2=np.asarray(gla_wa2, np.float32),
                  gla_ba=np.asarray(gla_ba, np.float32), gla_norm_g=np.asarray(gla_norm_g, np.float32))
    w_out = np.asarray(w_out, np.float32)
    rope, consts = rope_tables(), const_tables()
    if "a" not in _CACHE:
        _CACHE["a"] = build_a()[0]
        _CACHE["b"] = build_b(S // 2, False)[0]
        _CACHE["bf"] = build_b(S // 2, True)[0]
    nca, ncb, ncbf = _CACHE["a"], _CACHE["b"], _CACHE["bf"]
    B_ = inputs["x"].shape[0]
    xcur = inputs["x"]
    fg = np.ascontiguousarray(np.broadcast_to(np.asarray(final_norm_g, np.float32).reshape(1, D), (128, D)))
    cores = list(range(8))
    for l in range(DEPTH):
        cur = dict(inputs)
        cur["x"] = xcur
        per_hh = [layer_inputs_a(cur, l, 0, hh, rope, consts) for hh in range(2)]
        maps = []
        for c in cores:
            b, hh = c // 2, c % 2
            m = dict(per_hh[hh])
            m["x"] = np.ascontiguousarray(xcur[b])
            maps.append(m)
        res = run_bass_kernel_spmd(nca, maps, core_ids=cores)
        wo = wout_perm(w_out[l])
        maps = []
        for c in cores:
            b, th = c // 2, c % 2
            mixf = np.concatenate([res.results[2 * b]["mix"], res.results[2 * b + 1]["mix"]], axis=0)
            m = {"x": np.ascontiguousarray(xcur[b, th * 1024:(th + 1) * 1024]),
                 "mixf": np.ascontiguousarray(mixf[:, :, th * 1024:(th + 1) * 1024]),
                 "wout": wo, "consts": consts}
            if l == DEPTH - 1:
                m["fg"] = fg
            maps.append(m)
        if l < DEPTH - 1:
            resb = run_bass_kernel_spmd(ncb, maps, core_ids=cores)
            xcur = np.stack([np.concatenate([resb.results[2 * b]["xo"], resb.results[2 * b + 1]["xo"]], axis=0) for b in range(B_)])
        else:
            resb = run_bass_kernel_spmd(ncbf, maps, core_ids=cores)
            out = np.stack([np.concatenate([resb.results[2 * b]["y"], resb.results[2 * b + 1]["y"]], axis=0) for b in range(B_)])
    return out.astype(np.float32)


F_INPUTS = (("x", [S, D]), ("wblk", [DEPTH, NBLK, 128, NKC, 128]), ("wv", [DEPTH, 4, 128, NKC, 256]), ("wf", [DEPTH, 128, NKC, 2]),
            ("gbc", [DEPTH, 128, D]), ("fb", [DEPTH, 128, 2]), ("lam", [DEPTH, 128, 256]), ("lsc", [DEPTH, 128, 8]),
            ("dng", [DEPTH, 128, 1]), ("gng", [DEPTH, 128, 1]), ("gba", [DEPTH, 128, 1]), ("wa2", [DEPTH, 16, 128]),
            ("wout", [DEPTH, 16, 128, D]), ("fg", [128, D]), ("rope", [4, 128, S]), ("consts", [128, 384]))
PAIRS = [[0, 1], [2, 3], [4, 5], [6, 7]]


def emit_b_fused(p, l, x_ap, t_xin, mixfull, t_mixfull, wout_ap, xo_ap, t_xo, y_ap):
    s, nc = p.s, p.nc
    T = 1024
    wo = p.wo
    mf = nc.alloc_sbuf_tensor_at(f"mf{l}", [128, 16, T], BF16, offset=p.mix_off)
    yt = nc.alloc_sbuf_tensor_at(f"yt{l}", [128, D], F32, offset=p.g0 + 16384)
    if not hasattr(p, "t_wo"):
        p.t_wo = s.toks("wo", 16)
        p.t_mf = s.toks("mf", 16)
    t_wo, t_mf = p.t_wo, p.t_mf
    alias = [p.t_hT] + p.t_mix + t_wo + t_mf
    s.op("dve", lambda e: e.memset(p.sm[:, 20:21], 0.0), W=p.t_mix + t_mf + [p.t_sm])
    for half in range(2):
        t0 = half * T
        for c in range(16):
            r_, ch_ = c // 8, c % 8
            src = mixfull[ch_ // 4].ap()[r_ * 512 + (ch_ % 4) * 128:r_ * 512 + (ch_ % 4 + 1) * 128, t0:t0 + T]
            s.dma("sp", mf[:, c, :], src, t_mf[c], R=[t_mixfull[ch_ // 4]], W=[t_mf[c]])
        for tb in range(T // 128):
            i = tb % 2
            xt, txt = p.xt[i], p.t_xt[i]
            r0 = t0 + tb * 128
            s.dma("sp", xt[:], x_ap[r0:r0 + 128, :], txt, R=([t_xin] if t_xin is not None else []), W=[txt])
            for fc in range(4):
                b = p.next_bank()
                for c in range(16):
                    s.op("pe", lambda e: e.matmul(p.B[b][:], lhsT=mf[:, c, tb * 128:(tb + 1) * 128], rhs=wo[:, c, fc * 512:(fc + 1) * 512],
                                                  start=(c == 0), stop=(c == 15)),
                         R=[t_wo[c], t_mf[c]], W=[p.t_B[b]])
                s.op("dve", lambda e: e.tensor_tensor(out=xt[:, fc * 512:(fc + 1) * 512], in0=xt[:, fc * 512:(fc + 1) * 512], in1=p.B[b][:], op=ALU.add),
                     R=[p.t_B[b], txt], W=[txt])
            if xo_ap is not None:
                s.dma("sp", xo_ap[r0:r0 + 128, :], xt[:], txt, R=[txt], W=[t_xo])
            if y_ap is not None:
                ss = p.sm[:, 0:1]
                s.op("act", lambda e: e.activation(out=yt[:], in_=xt[:], func=AF.Square, accum_out=ss), R=[txt], W=[p.t_v, p.t_sm])
                s.op("act", lambda e: e.activation(out=p.sm[:, 1:2], in_=ss, func=AF.Sqrt, bias=p.epsc[:, 0:1], scale=1.0 / D),
                     R=[p.t_sm, p.t_const], W=[p.t_sm])
                s.op("dve", lambda e: e.reciprocal(out=p.sm[:, 2:3], in_=p.sm[:, 1:2]), R=[p.t_sm], W=[p.t_sm])
                s.op("dve", lambda e: e.scalar_tensor_tensor(out=yt[:], in0=xt[:], scalar=p.sm[:, 2:3], in1=p.gbc[:], op0=ALU.mult, op1=ALU.mult),
                     R=[txt, p.t_sm, p.t_gbc], W=[p.t_v])
                s.dma("sp", y_ap[r0:r0 + 128, :], yt[:], p.t_v, R=[p.t_v], W=[p.t_out])
    s.op("dve", lambda e: e.memset(p.sm[:, 20:21], 0.0), W=alias + [p.t_sm])


def build_fused(depth=DEPTH):
    nc = bass.Bass("TRN2", target_bir_lowering=False)
    aps = {n: _dram(nc, n, ([depth] + list(shp[1:])) if (len(shp) > 2 and shp[0] == DEPTH and n not in ("rope",)) else shp) for n, shp in F_INPUTS}
    y = _dram(nc, "y", [S, D], kind="ExternalOutput")
    xs = [nc.dram_tensor(f"xs{i}", [S, D], F32).ap() for i in range(2)]
    mo = [nc.dram_tensor(f"mixown{i}", [4 * 128, S], BF16) for i in range(2)]
    mfu = [nc.dram_tensor(f"mixfull{i}", [2 * 4 * 128, S], BF16) for i in range(2)]
    p = Prog(nc)
    s = p.s
    t_xs = s.toks("xs", 2)
    t_mo, t_mfull = s.toks("mixown", 2), s.toks("mixfull", 2)
    p.wo = nc.alloc_sbuf_tensor_at("wo", [128, 16, D], BF16, offset=p.hT_off)
    p.t_wo = s.toks("wo", 16)
    p.t_mf = s.toks("mf", 16)
    p.load_consts(aps["consts"])
    x_in, t_xin = aps["x"], None
    for l in range(depth):
        W = {k: aps[k][l] for k in ("wblk", "wv", "wf", "gbc", "fb", "lam", "lsc", "dng", "gng", "gba", "wa2")}

        def hook(g):
            for ch in (2 * g, 2 * g + 1):
                hs = ch // 4
                s.dma("sp", mo[hs].ap()[(ch % 4) * 128:(ch % 4 + 1) * 128, :], p.mixT[:, ch, :], p.t_mix[ch], R=[p.t_mix[ch]], W=[t_mo[hs]])
            if g % 2 == 1:
                hs = g // 2
                s.coll(lambda e: e.collective_compute("AllGather", ALU.bypass, replica_groups=PAIRS,
                                                      ins=[mo[hs].ap().opt()], outs=[mfu[hs].ap().opt()]),
                       R=[t_mo[hs]], W=[t_mfull[hs]])
        def pre_b(l=l):
            for c in range(16):
                s.dma("pool", p.wo[:, c, :], aps["wout"][l][c], p.t_wo[c], W=[p.t_wo[c]] + ([p.t_hT] if c == 0 else []))
        p.pre_b = pre_b
        p.program_a(x_in, W, aps["rope"], None, t_x=t_xin, mixd=hook)
        last = (l == depth - 1)
        if last:
            s.dma("sp", p.gbc[:], aps["fg"], p.t_gbc, W=[p.t_gbc])
        k = l % 2
        emit_b_fused(p, l, x_in, t_xin, mfu, t_mfull, aps["wout"][l],
                     None if last else xs[k], None if last else t_xs[k], y if last else None)
        x_in, t_xin = xs[k], t_xs[k]
    p.s.finish([p.t_out])
    return nc, p


def fused_inputs(inputs, w_out, final_norm_g, b, hh, rope, consts, packs):
    m = {k: packs[hh][k] for k in packs[hh]}
    m["x"] = np.ascontiguousarray(inputs["x"][b])
    m["rope"] = rope
    m["consts"] = consts
    return m


def kernel(x, norm_g, w_in, fox_fb, diff_lam, diff_norm_g, gla_wa2, gla_ba, gla_norm_g, w_out, final_norm_g):
    inputs = dict(x=np.asarray(x, np.float32), norm_g=np.asarray(norm_g, np.float32), w_in=np.asarray(w_in, np.float32),
                  fox_fb=np.asarray(fox_fb, np.float32), diff_lam=np.asarray(diff_lam, np.float32),
                  diff_norm_g=np.asarray(diff_norm_g, np.float32), gla_wa2=np.asarray(gla_wa2, np.float32),
                  gla_ba=np.asarray(gla_ba, np.float32), gla_norm_g=np.asarray(gla_norm_g, np.float32))
    w_out = np.asarray(w_out, np.float32)
    rope, consts = rope_tables(), const_tables()
    if "f" not in _CACHE:
        _CACHE["f"] = build_fused()[0]
    nc = _CACHE["f"]
    fg = np.ascontiguousarray(np.broadcast_to(np.asarray(final_norm_g, np.float32).reshape(1, D), (128, D)))
    wo_all = np.stack([wout_perm(w_out[l]) for l in range(DEPTH)])
    packs = []
    for hh in range(2):
        per_l = [layer_inputs_a(inputs, l, 0, hh, rope, consts) for l in range(DEPTH)]
        pk = {k: np.stack([per_l[l][k] for l in range(DEPTH)]) for k in ("wblk", "wv", "wf", "gbc", "fb", "lam", "lsc", "dng", "gng", "gba", "wa2")}
        pk["wout"] = wo_all
        pk["fg"] = fg
        packs.append(pk)
    cores = list(range(8))
    maps = [fused_inputs(inputs, w_out, final_norm_g, c // 2, c % 2, rope, consts, packs) for c in cores]
    res = run_bass_kernel_spmd(nc, maps, core_ids=cores)
    out = np.stack([res.results[2 * b]["y"] for b in range(inputs["x"].shape[0])])
    return out.astype(np.float32)


HD = D // 2
F2_INPUTS = (("x", [S, D]), ("xh", [S, HD]), ("wblk", [DEPTH, NBLK, 128, NKC, 128]), ("wv", [DEPTH, 4, 128, NKC, 256]), ("wf", [DEPTH, 128, NKC, 2]),
             ("gbc", [DEPTH, 128, D]), ("fb", [DEPTH, 128, 2]), ("lam", [DEPTH, 128, 256]), ("lsc", [DEPTH, 128, 8]),
             ("dng", [DEPTH, 128, 1]), ("gng", [DEPTH, 128, 1]), ("gba", [DEPTH, 128, 1]), ("wa2", [DEPTH, 16, 128]),
             ("wout", [DEPTH, 16, 128, HD]), ("fg", [128, D]), ("rope", [4, 128, S]), ("consts", [128, 384]))


def emit_b_fs(p, l, xin_piece, t_xin_piece, mixfull, t_mixfull, xop, t_xop, xg, t_xg):
    s, nc = p.s, p.nc
    T = 1024
    wo = p.wo
    mf = nc.alloc_sbuf_tensor_at(f"mf{l}", [128, 16, T], BF16, offset=p.mix_off)
    t_wo, t_mf = p.t_wo, p.t_mf
    alias = [p.t_hT] + p.t_mix + t_wo + t_mf
    s.op("dve", lambda e: e.memset(p.sm[:, 20:21], 0.0), W=p.t_mix + t_mf + [p.t_sm])
    for half in range(2):
        t0 = half * T
        for c in range(16):
            r_, ch_ = c // 8, c % 8
            src = mixfull[ch_ // 4].ap()[r_ * 512 + (ch_ % 4) * 128:r_ * 512 + (ch_ % 4 + 1) * 128, t0:t0 + T]
            s.dma("sp", mf[:, c, :], src, t_mf[c], R=[t_mixfull[ch_ // 4]], W=[t_mf[c]])
        for tbh in range(T // 128):
            tb = half * 8 + tbh
            i = tb % 2
            xt, txt = p.xt[i], p.t_xt[i]
            k, off = tb // 4, (tb % 4) * 128
            src_ap, src_tok = xin_piece(k, off)
            s.dma("sp", xt[:, 0:HD], src_ap, txt, R=([src_tok] if src_tok is not None else []), W=[txt])
            for fc in range(2):
                b = p.next_bank()
                for c in range(16):
                    s.op("pe", lambda e: e.matmul(p.B[b][:], lhsT=mf[:, c, tbh * 128:(tbh + 1) * 128], rhs=wo[:, c, fc * 512:(fc + 1) * 512],
                                                  start=(c == 0), stop=(c == 15)),
                         R=[t_wo[c], t_mf[c]], W=[p.t_B[b]])
                s.op("dve", lambda e: e.tensor_tensor(out=xt[:, fc * 512:(fc + 1) * 512], in0=xt[:, fc * 512:(fc + 1) * 512], in1=p.B[b][:], op=ALU.add),
                     R=[p.t_B[b], txt], W=[txt])
            s.dma("sp", xop[k].ap()[off:off + 128, :], xt[:, 0:HD], txt, R=[txt], W=[t_xop[k]])
            if tb % 4 == 3:
                s.coll(lambda e: e.collective_compute("AllGather", ALU.bypass, replica_groups=PAIRS,
                                                      ins=[xop[k].ap().opt()], outs=[xg[k].ap().opt()]),
                       R=[t_xop[k]], W=[t_xg[k]])
    s.op("dve", lambda e: e.memset(p.sm[:, 20:21], 0.0), W=alias + [p.t_sm])


def emit_final_norm(p, xg, t_xg, fg_ap, y_ap):
    s, nc = p.s, p.nc
    yt = nc.alloc_sbuf_tensor_at("ytf", [128, D], F32, offset=p.g0 + 16384)
    s.op("dve", lambda e: e.memset(p.sm[:, 21:22], 0.0), W=[p.t_v, p.t_sm] + p.t_hrows)
    s.dma("sp", p.gbc[:], fg_ap, p.t_gbc, W=[p.t_gbc])
    for tb in range(NTB):
        i = tb % 2
        xt, txt = p.xt[i], p.t_xt[i]
        k, off = tb // 4, (tb % 4) * 128
        for r in range(2):
            s.dma("sp", xt[:, r * HD:(r + 1) * HD], xg[k].ap()[r * 512 + off:r * 512 + off + 128, :], txt, R=[t_xg[k]], W=[txt])
        st, tst = p.sm[:, 24 + 4 * i:28 + 4 * i], p.t_st[i]
        s.op("act", lambda e: e.activation(out=yt[:], in_=xt[:], func=AF.Square, accum_out=st[:, 0:1]), R=[txt], W=[p.t_v, tst])
        s.op("act", lambda e: e.activation(out=st[:, 1:2], in_=st[:, 0:1], func=AF.Sqrt, bias=p.epsc[:, 0:1], scale=1.0 / D),
             R=[tst, p.t_const], W=[tst])
        s.op("dve", lambda e: e.reciprocal(out=st[:, 2:3], in_=st[:, 1:2]), R=[tst], W=[tst])
        s.op("dve", lambda e: e.scalar_tensor_tensor(out=yt[:], in0=xt[:], scalar=st[:, 2:3], in1=p.gbc[:], op0=ALU.mult, op1=ALU.mult),
             R=[txt, tst, p.t_gbc], W=[p.t_v])
        s.dma("sp", y_ap[tb * 128:(tb + 1) * 128, :], yt[:], p.t_v, R=[p.t_v], W=[p.t_out])


def build_fused2(depth=DEPTH):
    nc = bass.Bass("TRN2", target_bir_lowering=False)
    aps = {n: _dram(nc, n, ([depth] + list(shp[1:])) if (len(shp) > 2 and shp[0] == DEPTH and n != "rope") else shp) for n, shp in F2_INPUTS}
    y = _dram(nc, "y", [S, D], kind="ExternalOutput")
    mo = [nc.dram_tensor(f"mixown{i}", [4 * 128, S], BF16) for i in range(2)]
    mfu = [nc.dram_tensor(f"mixfull{i}", [2 * 4 * 128, S], BF16) for i in range(2)]
    xop = [[nc.dram_tensor(f"xop{q}_{k}", [512, HD], F32) for k in range(4)] for q in range(2)]
    xg = [nc.dram_tensor(f"xg{k}", [2 * 512, HD], F32) for k in range(4)]
    p = Prog(nc)
    s = p.s
    t_xop = [s.toks(f"xop{q}_", 4) for q in range(2)]
    t_xg = s.toks("xg", 4)
    t_mo, t_mfull = s.toks("mixown", 2), s.toks("mixfull", 2)
    p.wo = nc.alloc_sbuf_tensor_at("wo", [128, 16, HD], BF16, offset=p.hT_off)
    p.t_wo = s.toks("wo", 16)
    p.t_mf = s.toks("mf", 16)
    p.load_consts(aps["consts"])

    def x_from_xg(tb):
        k, off = tb // 4, (tb % 4) * 128
        return [(slice(r * HD, (r + 1) * HD), xg[k].ap()[r * 512 + off:r * 512 + off + 128, :], t_xg[k]) for r in range(2)]

    for l in range(depth):
        W = {k: aps[k][l] for k in ("wblk", "wv", "wf", "gbc", "fb", "lam", "lsc", "dng", "gng", "gba", "wa2")}

        def hook(g):
            for ch in (2 * g, 2 * g + 1):
                hs = ch // 4
                s.dma("sp", mo[hs].ap()[(ch % 4) * 128:(ch % 4 + 1) * 128, :], p.mixT[:, ch, :], p.t_mix[ch], R=[p.t_mix[ch]], W=[t_mo[hs]])
            if g % 2 == 1:
                hs = g // 2
                s.coll(lambda e: e.collective_compute("AllGather", ALU.bypass, replica_groups=PAIRS,
                                                      ins=[mo[hs].ap().opt()], outs=[mfu[hs].ap().opt()]),
                       R=[t_mo[hs]], W=[t_mfull[hs]])

        def pre_b(l=l):
            for c in range(16):
                s.dma("pool", p.wo[:, c, :], aps["wout"][l][c], p.t_wo[c], W=[p.t_wo[c]] + ([p.t_hT] if c == 0 else []))
        p.pre_b = pre_b
        p.program_a(aps["x"] if l == 0 else x_from_xg, W, aps["rope"], None, mixd=hook)
        q = l % 2
        if l == 0:
            xin = lambda k, off: (aps["xh"][k * 512 + off:k * 512 + off + 128, :], None)
        else:
            xin = lambda k, off, q=q: (xop[1 - q][k].ap()[off:off + 128, :], t_xop[1 - q][k])
        emit_b_fs(p, l, xin, None, mfu, t_mfull, xop[q], t_xop[q], xg, t_xg)
    emit_final_norm(p, xg, t_xg, aps["fg"], y)
    p.s.finish([p.t_out])
    return nc, p


def kernel(x, norm_g, w_in, fox_fb, diff_lam, diff_norm_g, gla_wa2, gla_ba, gla_norm_g, w_out, final_norm_g):
    inputs = dict(x=np.asarray(x, np.float32), norm_g=np.asarray(norm_g, np.float32), w_in=np.asarray(w_in, np.float32),
                  fox_fb=np.asarray(fox_fb, np.float32), diff_lam=np.asarray(diff_lam, np.float32),
                  diff_norm_g=np.asarray(diff_norm_g, np.float32), gla_wa2=np.asarray(gla_wa2, np.float32),
                  gla_ba=np.asarray(gla_ba, np.float32), gla_norm_g=np.asarray(gla_norm_g, np.float32))
    w_out = np.asarray(w_out, np.float32)
    rope, consts = rope_tables(), const_tables()
    if "f2" not in _CACHE:
        _CACHE["f2"] = build_fused2()[0]
    nc = _CACHE["f2"]
    fg = np.ascontiguousarray(np.broadcast_to(np.asarray(final_norm_g, np.float32).reshape(1, D), (128, D)))
    wo_all = np.stack([wout_perm(w_out[l]) for l in range(DEPTH)])
    packs = []
    for hh in range(2):
        per_l = [layer_inputs_a(inputs, l, 0, hh, rope, consts) for l in range(DEPTH)]
        pk = {k: np.stack([per_l[l][k] for l in range(DEPTH)]) for k in ("wblk", "wv", "wf", "gbc", "fb", "lam", "lsc", "dng", "gng", "gba", "wa2")}
        pk["wout"] = np.ascontiguousarray(wo_all[:, :, :, hh * HD:(hh + 1) * HD])
        pk["fg"] = fg
        packs.append(pk)
    cores = list(range(8))
    maps = []
    for c in cores:
        b, hh = c // 2, c % 2
        m = dict(packs[hh])
        m["x"] = np.ascontiguousarray(inputs["x"][b])
        m["xh"] = np.ascontiguousarray(inputs["x"][b][:, hh * HD:(hh + 1) * HD])
        m["rope"] = rope
        m["consts"] = consts
        maps.append(m)
    res = run_bass_kernel_spmd(nc, maps, core_ids=cores)
    out = np.stack([res.results[2 * b]["y"] for b in range(inputs["x"].shape[0])])
    return out.astype(np.float32)
```

```python
import math
import numpy as np
import concourse.bass as bass
import concourse.mybir as mybir
from concourse.bass_utils import run_bass_kernel_spmd

F32 = mybir.dt.float32
BF16 = mybir.dt.bfloat16
AF = mybir.ActivationFunctionType
ALU = mybir.AluOpType
AX = mybir.AxisListType

D = 2048
S = 2048
DEPTH = 4
D_IN = 7700
NTB = 16
NKC = 16
RMS_EPS = 1e-6
ROPE_THETA = 500000.0
NEG = -30000.0

OFF = {}
_o = 0
for _n, _w in (("fox_q", 512), ("fox_k", 512), ("fox_v", 512), ("fox_f", 4), ("fox_g", 512),
               ("diff_q", 512), ("diff_k", 512), ("diff_v", 512), ("diff_g", 512),
               ("moba_q", 512), ("moba_k", 512), ("moba_v", 512), ("moba_g", 512),
               ("gla_q", 256), ("gla_k", 256), ("gla_v", 512), ("gla_a", 16), ("gla_g", 512)):
    OFF[_n] = _o
    _o += _w
assert _o == D_IN

BLK = ["fq0", "fq1", "fk0", "fk1", "fg0", "fg1",
       "dq0", "dqs0", "dq1", "dqs1", "dk0", "dks0", "dk1", "dks1", "dg0", "dg1",
       "mq0", "mqs0", "mq1", "mqs1", "mk0", "mks0", "mk1", "mks1", "mg0", "mg1",
       "ga", "gq", "gk", "gg0", "gg1"]
BIDX = {n: i for i, n in enumerate(BLK)}
NBLK = len(BLK)


def _swap_cols_diff():
    m = {}
    for base in (0, 64):
        for c in range(8):
            m[base + c] = base + c + 8
            m[base + 8 + c] = base + c
    return m


def _swap_cols_moba():
    m = {}
    for c in range(16):
        m[c] = c + 16
        m[16 + c] = c
    return m


def _block_cols(hh):
    cols = {}
    for j in range(2):
        h = 2 * hh + j
        r = np.arange(128)
        cols[f"fq{j}"] = OFF["fox_q"] + h * 128 + r
        cols[f"fk{j}"] = OFF["fox_k"] + h * 128 + r
        cols[f"fg{j}"] = OFF["fox_g"] + h * 128 + r
        for pre, seg, sw in (("d", "diff", _swap_cols_diff()), ("m", "moba", _swap_cols_moba())):
            for t in ("q", "k"):
                base = OFF[f"{seg}_{t}"] + h * 128
                cols[f"{pre}{t}{j}"] = base + r
                s = np.full(128, -1, dtype=np.int64)
                for c, src in sw.items():
                    s[c] = base + src
                cols[f"{pre}{t}s{j}"] = s
            cols[f"{pre}g{j}"] = OFF[f"{seg}_g"] + h * 128 + r
        cols[f"gg{j}"] = OFF["gla_g"] + h * 128 + r
    r64 = np.arange(64)
    cols["gq"] = np.concatenate([OFF["gla_q"] + (2 * hh) * 64 + r64, OFF["gla_q"] + (2 * hh + 1) * 64 + r64])
    cols["gk"] = np.concatenate([OFF["gla_k"] + (2 * hh) * 64 + r64, OFF["gla_k"] + (2 * hh + 1) * 64 + r64])
    a = np.full(128, -1, dtype=np.int64)
    a[:16] = OFF["gla_a"] + np.arange(16)
    cols["ga"] = a
    return cols


def pack_layer_weights(w_in_l, hh):
    cols = _block_cols(hh)
    idx = np.stack([cols[n] for n in BLK])
    wz = np.concatenate([w_in_l, np.zeros((D, 1), np.float32)], axis=1)
    idx = np.where(idx < 0, D_IN, idx)
    g = wz[:, idx.reshape(-1)].reshape(NKC, 128, NBLK, 128)
    wblk = np.ascontiguousarray(g.transpose(2, 1, 0, 3))
    vcols = np.stack([OFF[f"{seg}_v"] + hh * 256 + np.arange(256) for seg in ("fox", "diff", "moba", "gla")])
    gv = w_in_l[:, vcols.reshape(-1)].reshape(NKC, 128, 4, 256)
    wv = np.ascontiguousarray(gv.transpose(2, 1, 0, 3))
    fcols = OFF["fox_f"] + 2 * hh + np.arange(2)
    wf = np.ascontiguousarray(w_in_l[:, fcols].reshape(NKC, 128, 2).transpose(1, 0, 2))
    return wblk, wv, wf


def rope_tables():
    t = np.arange(S, dtype=np.float32)
    out = np.zeros((4, 128, S), np.float32)
    out[0] = 1.0
    out[2] = 1.0
    inv = (ROPE_THETA ** (-np.arange(0, 16, 2, dtype=np.float32) / 16)).astype(np.float32)
    ang = t[None, :] * inv[:, None]
    for base in (0, 64):
        out[0, base:base + 8] = np.cos(ang)
        out[0, base + 8:base + 16] = np.cos(ang)
        out[1, base:base + 8] = -np.sin(ang)
        out[1, base + 8:base + 16] = np.sin(ang)
    inv = (ROPE_THETA ** (-np.arange(0, 32, 2, dtype=np.float32) / 32)).astype(np.float32)
    ang = t[None, :] * inv[:, None]
    out[2, 0:16] = np.cos(ang)
    out[2, 16:32] = np.cos(ang)
    out[3, 0:16] = -np.sin(ang)
    out[3, 16:32] = np.sin(ang)
    return out


def const_tables():
    c = np.zeros((128, 384), np.float32)
    c[:, 0:128] = np.eye(128, dtype=np.float32)
    c[:, 128:256] = np.triu(np.ones((128, 128), np.float32))
    c[:, 256:384] = 1.0
    return c


class Tok:
    __slots__ = ("name", "w", "r", "dsem")

    def __init__(self, name):
        self.name = name
        self.w = None
        self.r = {}
        self.dsem = None


class Sched:
    def __init__(self, nc):
        self.nc = nc
        self.E = {"pe": nc.tensor, "act": nc.scalar, "dve": nc.vector, "pool": nc.gpsimd, "sp": nc.sync}
        self.sems = []
        self.cnt = []
        self.ekey = {}
        for e in ("pe", "act", "dve", "pool"):
            self.ekey[e] = self._newsem("e_" + e)
        self.seen = {e: {} for e in self.E}
        self.ninst = 0

    def _newsem(self, name):
        self.sems.append(self.nc.alloc_semaphore(name=name))
        self.cnt.append(0)
        return len(self.sems) - 1

    def tok(self, name):
        return Tok(name)

    def toks(self, name, n):
        return [Tok(f"{name}{i}") for i in range(n)]

    def _deps(self, R, W):
        deps = {}

        def add(k, v):
            if deps.get(k, 0) < v:
                deps[k] = v
        for t in R:
            if t.w is not None:
                add(*t.w)
        for t in W:
            if t.w is not None:
                add(*t.w)
            for k, v in t.r.items():
                add(k, v)
        return deps

    def _emit_waits(self, e, deps, skip_key=None):
        seen = self.seen[e]
        for k, v in deps.items():
            if k == skip_key:
                continue
            if seen.get(k, 0) < v:
                self.E[e].wait_ge(self.sems[k], v)
                seen[k] = v
                self.ninst += 1

    def op(self, e, fn, R=(), W=()):
        deps = self._deps(R, W)
        key = self.ekey[e]
        self._emit_waits(e, deps, skip_key=key if e == "pe" else None)
        ins = fn(self.E[e])
        ins.then_inc(self.sems[key], 1)
        self.cnt[key] += 1
        v = self.cnt[key]
        self.ninst += 1
        for t in R:
            if t.r.get(key, 0) < v:
                t.r[key] = v
        for t in W:
            t.w = (key, v)
            t.r = {}
        return ins

    def dma(self, q, out, in_, sb, R=(), W=()):
        if sb.dsem is None:
            sb.dsem = self._newsem("d_" + sb.name)
        key = sb.dsem
        deps = self._deps(R, W)
        if self.cnt[key] > 0:
            deps[key] = max(deps.get(key, 0), self.cnt[key])
        self._emit_waits(q, deps)
        ins = self.E[q].dma_start(out=out, in_=in_)
        ins.then_inc(self.sems[key], 16)
        self.cnt[key] += 16
        v = self.cnt[key]
        self.ninst += 1
        for t in R:
            if t.r.get(key, 0) < v:
                t.r[key] = v
        for t in W:
            t.w = (key, v)
            t.r = {}
        return ins

    def coll(self, fn, R=(), W=()):
        if not hasattr(self, "ckey"):
            self.ckey = self._newsem("coll")
        key = self.ckey
        deps = self._deps(R, W)
        if self.cnt[key] > 0:
            deps[key] = max(deps.get(key, 0), self.cnt[key])
        self._emit_waits("pool", deps)
        ins = fn(self.E["pool"])
        ins.then_inc(self.sems[key], 1)
        self.cnt[key] += 1
        v = self.cnt[key]
        self.ninst += 1
        for t in R:
            if t.r.get(key, 0) < v:
                t.r[key] = v
        for t in W:
            t.w = (key, v)
            t.r = {}
        return ins

    def finish(self, toks):
        deps = {}
        for t in toks:
            if t.w is not None and deps.get(t.w[0], 0) < t.w[1]:
                deps[t.w[0]] = t.w[1]
        self._emit_waits("sp", deps)


class SB:
    def __init__(self, nc):
        self.nc = nc
        self.off = 16384
        self.t = {}

    def alloc(self, name, shape, dt, at=None):
        nbytes = int(np.prod(shape[1:])) * (2 if dt == BF16 else 4)
        if at is None:
            at = self.off
            self.off = (at + nbytes + 63) // 64 * 64
        h = self.nc.alloc_sbuf_tensor_at(name, list(shape), dt, offset=at)
        self.t[name] = h
        return h, at, nbytes


class Prog:
    def __init__(self, nc):
        self.nc = nc
        self.s = Sched(nc)
        self.sb = SB(nc)
        self._alloc()

    def _alloc(self):
        sb, s = self.sb, self.s
        A = sb.alloc
        self.hT, self.hT_off, _ = A("hT", [128, NKC, S], BF16)
        self.t_hT = s.tok("hT")
        self.qT, g0, _ = A("qT", [128, 2, S], BF16)
        self.g0 = g0
        self.kT, _, _ = A("kT", [128, 2, S], BF16)
        self.v, _, _ = A("v", [128, NTB, 256], BF16)
        self.sg, _, _ = A("sg", [128, 2, S], BF16)
        self.t_qT, self.t_kT, self.t_v, self.t_sg = s.tok("qT"), s.tok("kT"), s.tok("v"), s.tok("sg")
        self.xt = [sb.alloc(f"xt{i}", [128, D], F32, at=g0 + i * 8192)[0] for i in range(2)]
        self.hrow, _, _ = sb.alloc("hrow", [128, D], BF16, at=g0 + 16384)
        self.junk, _, _ = sb.alloc("junk", [128, D], BF16, at=g0 + 20480)
        self.gbc, _, _ = sb.alloc("gbc", [128, D], F32, at=g0 + 24576)
        self.t_xt = [self.t_qT, self.t_kT]
        self.t_hrow = self.t_v
        self.t_junk = self.t_v
        self.t_gbc = self.t_sg
        self.scr_off = sb.off
        sb.off += 20480
        o = self.scr_off
        self.eg, _, _ = sb.alloc("eg", [128, S], BF16, at=o)
        self.eng, _, _ = sb.alloc("eng", [128, S], BF16, at=o + 4096)
        self.eend, _, _ = sb.alloc("eend", [128, S], BF16, at=o + 8192)
        self.gcs, _, _ = sb.alloc("gcs", [128, S], F32, at=o + 12288)
        self.negB, _, _ = sb.alloc("negB", [128, 8, 8, 128], BF16, at=o)
        self.t_scr = s.tok("scr")
        self.mixT, self.mix_off, _ = A("mixT", [128, 8, S], BF16)
        self.t_mix = s.toks("mix", 8)
        self.NW = 4
        self.wb = [A(f"wb{i}", [128, NKC, 128], BF16)[0] for i in range(self.NW)]
        self.t_wb = s.toks("wb", self.NW)
        self.wvb, _, _ = A("wvb", [128, NKC, 256], BF16)
        self.t_wvb = s.tok("wvb")
        self.wfb, _, _ = A("wfb", [128, NKC, 2], BF16)
        self.t_wfb = s.tok("wfb")
        self.NPT = 4
        self.pt = [A(f"pt{i}", [128, 512], BF16)[0] for i in range(self.NPT)]
        self.t_pt = s.toks("pt", self.NPT)
        self.ft_off = sb.off
        self.ft = [A(f"ft{i}", [128, 512], F32)[0] for i in range(4)]
        self.t_ft = s.toks("ft", 4)
        self.sq, _, _ = A("sq", [128, 512], BF16)
        self.t_sq = s.tok("sq")
        self.rt = [A(f"rt{i}", [128, 2, 512], F32)[0] for i in range(2)]
        self.t_rt = s.toks("rt", 2)
        self.ident, _, _ = A("ident", [128, 128], BF16)
        self.tri, _, _ = A("tri", [128, 128], BF16)
        self.onesb, _, _ = A("onesb", [128, 128], BF16)
        self.negtri, _, _ = A("negtri", [128, 128], BF16)
        self.cf32, _, _ = A("cf32", [128, 384], F32)
        self.t_const = s.tok("const")
        self.epsc, _, _ = A("epsc", [128, 2], F32)
        self.t_out = s.tok("out")
        self.fb, _, _ = A("fb", [128, 2], F32)
        self.lam, _, _ = A("lam", [128, 256], F32)
        self.lsc, _, _ = A("lsc", [128, 8], F32)
        self.dng, _, _ = A("dng", [128, 1], F32)
        self.gng, _, _ = A("gng", [128, 1], F32)
        self.gba, _, _ = A("gba", [128, 1], F32)
        self.wa2, _, _ = A("wa2", [16, 128], BF16)
        self.t_par = s.tok("par")
        self.t_wa2 = s.tok("wa2")
        self.sm, _, _ = A("sm", [128, 64], F32)
        self.t_sm = s.tok("sm")
        self.sm2, _, _ = A("sm2", [128, 512], F32)
        self.t_sm2 = s.tok("sm2")
        self.sm3, _, _ = A("sm3", [128, 256], F32)
        self.t_sm3 = s.tok("sm3")
        self.aT = self.kT[0:16, 1, :]
        self.t_aT = self.t_kT
        self.S32, _, _ = A("S32", [128, 128], F32)
        self.S16, _, _ = A("S16", [128, 2, 128], BF16)
        self.t_S32, self.t_S16 = s.tok("S32"), s.tok("S16")
        self.ktok, _, _ = A("ktok", [128, NTB, 128], BF16)
        self.t_ktok = s.tok("ktok")
        self.at, _, _ = A("at", [128, 2, 128], BF16)
        self.t_at = s.tok("at")
        self.at2, _, _ = A("at2", [128, 2, 128], BF16)
        self.t_at2 = s.tok("at2")
        self.S16b, _, _ = A("S16b", [128, 2, 128], BF16)
        self.t_S16b = s.tok("S16b")
        self.kmT, _, _ = A("kmT", [128, 8], BF16)
        self.t_kmT = s.tok("kmT")
        self.nsel, _, _ = A("nsel", [128, 8, 8], BF16)
        self.t_nsel = s.tok("nsel")
        assert sb.off <= 229376, sb.off
        self.B = [self.nc.alloc_psum_tensor(f"B{i}", [128, 512], F32) for i in range(7)]
        self.t_B = s.toks("B", 7)
        self.T = self.nc.alloc_psum_tensor("T", [128, 1024], BF16)
        self.t_T = s.tok("T")
        self._pb = 0
        self._wi = 0

    def load_consts(self, consts_ap):
        s = self.s
        s.dma("sp", self.cf32[:], consts_ap, self.t_const, W=[self.t_const])
        s.op("dve", lambda e: e.tensor_copy(out=self.ident[:], in_=self.cf32[:, 0:128]), R=[self.t_const], W=[self.t_const])
        s.op("dve", lambda e: e.tensor_copy(out=self.tri[:], in_=self.cf32[:, 128:256]), R=[self.t_const], W=[self.t_const])
        s.op("dve", lambda e: e.tensor_copy(out=self.onesb[:], in_=self.cf32[:, 256:384]), R=[self.t_const], W=[self.t_const])
        s.op("dve", lambda e: e.tensor_scalar(out=self.negtri[:], in0=self.cf32[:, 128:256], scalar1=-1.0, scalar2=-NEG, op0=ALU.add, op1=ALU.mult),
             R=[self.t_const], W=[self.t_const])
        s.op("dve", lambda e: e.memset(self.epsc[:, 0:1], RMS_EPS), W=[self.t_const])
        s.op("dve", lambda e: e.memset(self.epsc[:, 1:2], 1.0), W=[self.t_const])

    def load_params(self, p):
        s = self.s
        t = self.t_par
        s.dma("sp", self.fb[:], p["fb"], t, W=[t])
        s.dma("sp", self.lam[:], p["lam"], t, W=[t])
        s.dma("sp", self.lsc[:], p["lsc"], t, W=[t])
        s.dma("sp", self.dng[:], p["dng"], t, W=[t])
        s.dma("sp", self.gng[:], p["gng"], t, W=[t])
        s.dma("sp", self.gba[:], p["gba"], t, W=[t])
        s.dma("pool", self.wa2[:], p["wa2"], self.t_wa2, W=[self.t_wa2])

    def next_bank(self, n=6):
        i = self._pb % n
        self._pb += 1
        return i

    def norm_phase(self, x_ap, g_ap, t_x=None):
        s = self.s
        if not hasattr(self, "t_st"):
            self.t_st = s.toks("st", 2)
            self.hrows = [self.hrow, self.junk]
            self.t_hrows = [s.tok("hrow0"), s.tok("hrow1")]
            self.junk2 = self.nc.alloc_sbuf_tensor_at("junk2", [128, D], BF16, offset=self.ft_off)
            self.Tb = [self.T[:, :], self.B[5][:, :].bitcast(BF16), self.B[6][:, :].bitcast(BF16)]
            self.t_Tb = [self.t_T, self.t_B[5], self.t_B[6]]
        s.op("dve", lambda e: e.memset(self.sm[:, 21:22], 0.0), W=[self.t_v, self.t_sm] + self.t_hrows)
        s.dma("sp", self.gbc[:], g_ap, self.t_gbc, W=[self.t_gbc])
        nT = 0
        for tb in range(NTB):
            i = tb % 2
            xt, txt = self.xt[i], self.t_xt[i]
            hr, thr = self.hrows[i], self.t_hrows[i]
            st, tst = self.sm[:, 24 + 4 * i:28 + 4 * i], self.t_st[i]
            if callable(x_ap):
                for (cs_, src_, tk_) in x_ap(tb):
                    s.dma("sp", xt[:, cs_], src_, txt, R=[tk_], W=[txt])
            else:
                s.dma("sp", xt[:], x_ap[tb * 128:(tb + 1) * 128, :], txt, R=([t_x] if t_x is not None else []), W=[txt])
            s.op("act", lambda e: e.activation(out=self.junk2[:], in_=xt[:], func=AF.Square, accum_out=st[:, 0:1]),
                 R=[txt], W=[self.t_ft[0], self.t_ft[1], tst])
            s.op("act", lambda e: e.activation(out=st[:, 1:2], in_=st[:, 0:1], func=AF.Sqrt, bias=self.epsc[:, 0:1], scale=1.0 / D),
                 R=[tst, self.t_const], W=[tst])
            s.op("dve", lambda e: e.reciprocal(out=st[:, 2:3], in_=st[:, 1:2]), R=[tst], W=[tst])
            s.op("dve", lambda e: e.scalar_tensor_tensor(out=hr[:], in0=xt[:], scalar=st[:, 2:3], in1=self.gbc[:],
                                                         op0=ALU.mult, op1=ALU.mult),
                 R=[txt, tst, self.t_gbc], W=[thr])
            for half in range(2):
                Tb, tTb = self.Tb[nT % 3], self.t_Tb[nT % 3]
                nT += 1
                for k in range(8):
                    kc = half * 8 + k
                    s.op("pe", lambda e: e.transpose(out=Tb[:, k * 128:(k + 1) * 128], in_=hr[:, kc * 128:(kc + 1) * 128],
                                                     identity=self.ident[:]),
                         R=[thr, self.t_const], W=[tTb])
                src = Tb.rearrange("p (k t) -> p k t", t=128)
                dst = self.hT[:, half * 8:(half + 1) * 8, tb * 128:(tb + 1) * 128]
                if half == 0:
                    s.op("act", lambda e: e.copy(out=dst, in_=src), R=[tTb], W=[self.t_hT])
                else:
                    s.op("dve", lambda e: e.tensor_copy(out=dst, in_=src), R=[tTb], W=[self.t_hT])
        s.op("dve", lambda e: e.memset(self.sm[:, 21:22], 0.0), W=[self.t_v, self.t_sm] + self.t_hrows)

    def load_wblk(self, wblk_ap, name):
        i = self._wi % self.NW
        self._wi += 1
        self.s.dma("pool", self.wb[i][:], wblk_ap[BIDX[name]], self.t_wb[i], W=[self.t_wb[i]])
        return self.wb[i], self.t_wb[i]

    def proj_chunk(self, w, tw, tc, bank, M=128):
        s = self.s
        for kc in range(NKC):
            s.op("pe", lambda e, kc=kc: e.matmul(self.B[bank][0:M, :], lhsT=w[:, kc, 0:M],
                                                 rhs=self.hT[:, kc, tc * 512:(tc + 1) * 512],
                                                 start=(kc == 0), stop=(kc == NKC - 1)),
                 R=[tw, self.t_hT], W=[self.t_B[bank]])

    def proj_plain(self, wblk_ap, name, dst, tdst, silu=False, M=128, eng_alt=True):
        s = self.s
        w, tw = self.load_wblk(wblk_ap, name)
        for tc in range(4):
            b = self.next_bank()
            self.proj_chunk(w, tw, tc, b, M)
            d = dst[0:M, tc * 512:(tc + 1) * 512]
            if silu:
                s.op("act", lambda e: e.activation(out=d, in_=self.B[b][0:M, :], func=AF.Silu),
                     R=[self.t_B[b]], W=[tdst])
            elif eng_alt and tc % 2 == 1:
                s.op("dve", lambda e: e.tensor_copy(out=d, in_=self.B[b][0:M, :]), R=[self.t_B[b]], W=[tdst])
            else:
                s.op("act", lambda e: e.copy(out=d, in_=self.B[b][0:M, :]), R=[self.t_B[b]], W=[tdst])

    def proj_rope(self, wblk_ap, name, sname, dst, tdst, rope_ap, tbl):
        s = self.s
        w, tw = self.load_wblk(wblk_ap, name)
        ws, tws = self.load_wblk(wblk_ap, sname)
        for tc in range(4):
            r = tc % 2
            s.dma("sp", self.rt[r][:, 0, :], rope_ap[2 * tbl, :, tc * 512:(tc + 1) * 512], self.t_rt[r], W=[self.t_rt[r]])
            s.dma("sp", self.rt[r][:, 1, :], rope_ap[2 * tbl + 1, :, tc * 512:(tc + 1) * 512], self.t_rt[r], W=[self.t_rt[r]])
            b0 = self.next_bank()
            self.proj_chunk(w, tw, tc, b0)
            b1 = self.next_bank()
            self.proj_chunk(ws, tws, tc, b1)
            f0, f1 = self.ft[2 * r], self.ft[2 * r + 1]
            t0, t1 = self.t_ft[2 * r], self.t_ft[2 * r + 1]
            s.op("dve", lambda e: e.tensor_tensor(out=f0[:], in0=self.B[b0][:], in1=self.rt[r][:, 0, :], op=ALU.mult),
                 R=[self.t_B[b0], self.t_rt[r]], W=[t0])
            s.op("dve", lambda e: e.tensor_tensor(out=f1[:], in0=self.B[b1][:], in1=self.rt[r][:, 1, :], op=ALU.mult),
                 R=[self.t_B[b1], self.t_rt[r]], W=[t1])
            d = dst[:, tc * 512:(tc + 1) * 512]
            s.op("dve", lambda e: e.tensor_tensor(out=d, in0=f0[:], in1=f1[:], op=ALU.add), R=[t0, t1], W=[tdst])

    def proj_v(self, wv_ap, grp):
        s = self.s
        s.dma("pool", self.wvb[:], wv_ap[grp], self.t_wvb, W=[self.t_wvb])
        for tb in range(NTB):
            b = self.next_bank()
            for kc in range(NKC):
                s.op("pe", lambda e, kc=kc: e.matmul(self.B[b][:, 0:256], lhsT=self.hT[:, kc, tb * 128:(tb + 1) * 128],
                                                     rhs=self.wvb[:, kc, :], start=(kc == 0), stop=(kc == NKC - 1)),
                     R=[self.t_wvb, self.t_hT], W=[self.t_B[b]])
            if tb % 2 == 0:
                s.op("act", lambda e: e.copy(out=self.v[:, tb, :], in_=self.B[b][:, 0:256]), R=[self.t_B[b]], W=[self.t_v])
            else:
                s.op("dve", lambda e: e.tensor_copy(out=self.v[:, tb, :], in_=self.B[b][:, 0:256]), R=[self.t_B[b]], W=[self.t_v])

    def attention(self, units):
        s = self.s
        steps = []
        for u, (job, qc) in enumerate(units):
            nkb = 4 * qc + 4
            pair = (0, 1) if u % 2 == 0 else (2, 3)
            for kb in range(nkb):
                steps.append((job, qc, kb, kb == 0, kb == nkb - 1, pair))
        sbank = [5, 6, 4]

        def emit_qk(i):
            job, qc, kb, first, last, pair = steps[i]
            sb_ = sbank[i % 3]
            lo = max(0, kb - 4 * qc) * 128
            extra = [(jj, lhsT, self.ident[:], tk) for (jj, lhsT, tk) in (job["masks"](kb, qc) if job.get("masks") else [])]
            if kb >= 4 * qc:
                extra.append((kb - 4 * qc, self.ident[:], self.negtri[:], self.t_const))
            s.op("pe", lambda e: e.matmul(self.B[sb_][:, lo:512], lhsT=job["k"](kb), rhs=job["q"](qc * 512 + lo, (qc + 1) * 512),
                                          start=True, stop=(len(extra) == 0)),
                 R=job["R"], W=[self.t_B[sb_]])
            for n, (jj, lhsT, rhs, tk) in enumerate(extra):
                s.op("pe", lambda e: e.matmul(self.B[sb_][:, jj * 128:(jj + 1) * 128], lhsT=lhsT, rhs=rhs,
                                              start=False, stop=(n == len(extra) - 1)),
                     R=[tk, self.t_const], W=[self.t_B[sb_]])

        def emit_rest(i):
            job, qc, kb, first, last, pair = steps[i]
            sb_ = sbank[i % 3]
            lo = max(0, kb - 4 * qc) * 128
            p = i % self.NPT
            pt, tpt = self.pt[p], self.t_pt[p]
            bias = job["bias"](kb, qc) if job.get("bias") else None
            if bias is not None:
                s.op("act", lambda e: e.activation(out=pt[:, lo:512], in_=self.B[sb_][:, lo:512], func=AF.Exp, bias=bias, scale=job["scale"]),
                     R=[self.t_B[sb_]] + job["Rb"], W=[tpt])
            else:
                s.op("act", lambda e: e.activation(out=pt[:, lo:512], in_=self.B[sb_][:, lo:512], func=AF.Exp, scale=job["scale"]),
                     R=[self.t_B[sb_]], W=[tpt])
            ob, db = pair
            s.op("pe", lambda e: e.matmul(self.B[ob][:, lo:512], lhsT=job["v"](kb), rhs=pt[:, lo:512], start=first, stop=last),
                 R=[tpt] + job["Rv"], W=[self.t_B[ob]])
            s.op("pe", lambda e: e.matmul(self.B[db][:, lo:512], lhsT=self.onesb[:], rhs=pt[:, lo:512], start=first, stop=last),
                 R=[tpt, self.t_const], W=[self.t_B[db]])
            if last:
                job["epi"](qc, pair)

        n = len(steps)
        LA = 2
        for i in range(min(LA, n)):
            emit_qk(i)
        for i in range(n):
            if i + LA < n:
                emit_qk(i + LA)
            emit_rest(i)

    def mkjob(self, j, epi, rows=(0, 128), scale=128 ** -0.5, biasT=None, masks=None):
        lo_p, hi_p = rows
        d = dict(
            q=lambda lo, hi: self.qT[lo_p:hi_p, j, lo:hi],
            k=lambda kb: self.kT[lo_p:hi_p, j, kb * 128:(kb + 1) * 128],
            v=lambda kb: self.v[:, kb, j * 128:(j + 1) * 128],
            R=[self.t_qT, self.t_kT], Rv=[self.t_v], Rb=[self.t_sm2],
            scale=scale, epi=epi, masks=masks)
        if biasT is not None:
            d["bias"] = lambda kb, qc: biasT[:, j, qc, kb:kb + 1]
        return d

    def od_normalize(self, pair, fr, tr, fo, to):
        s = self.s
        ob, db = pair
        s.op("dve", lambda e: e.reciprocal(out=fr[:], in_=self.B[db][:]), R=[self.t_B[db]], W=[tr])
        s.op("dve", lambda e: e.tensor_tensor(out=fo[:], in0=self.B[ob][:], in1=fr[:], op=ALU.mult), R=[self.t_B[ob], tr], W=[to])

    def normalize_gate(self, src, tsrc, j, qc, gain, dst_ch, nb=4, scr=(0, 1)):
        s = self.s
        c0, c1 = qc * 512, (qc + 1) * 512
        s.op("act", lambda e: e.activation(out=self.sq[:], in_=src[:], func=AF.Square), R=[tsrc], W=[self.t_sq])
        s.op("pe", lambda e: e.matmul(self.B[nb][:], lhsT=self.onesb[:], rhs=self.sq[:], start=True, stop=True),
             R=[self.t_sq, self.t_const], W=[self.t_B[nb]])
        f4, t4 = self.ft[scr[0]], self.t_ft[scr[0]]
        s.op("act", lambda e: e.activation(out=f4[:], in_=self.B[nb][:], func=AF.Sqrt, bias=self.epsc[:, 0:1], scale=1.0 / 128),
             R=[self.t_B[nb], self.t_const], W=[t4])
        s.op("dve", lambda e: e.reciprocal(out=f4[:], in_=f4[:]), R=[t4], W=[t4])
        f5, t5 = self.ft[scr[1]], self.t_ft[scr[1]]
        s.op("dve", lambda e: e.tensor_tensor(out=f5[:], in0=src[:], in1=f4[:], op=ALU.mult), R=[tsrc, t4], W=[t5])
        s.op("dve", lambda e: e.scalar_tensor_tensor(out=self.mixT[:, dst_ch, c0:c1], in0=f5[:], scalar=gain, in1=self.sg[:, j, c0:c1],
                                                     op0=ALU.mult, op1=ALU.mult),
             R=[t5, self.t_sg, self.t_par, self.t_sm], W=[self.t_mix[dst_ch]])

    def fox(self, W):
        s = self.s
        wblk, wv, wf = W["wblk"], W["wv"], W["wf"]
        for j in range(2):
            self.proj_plain(wblk, f"fq{j}", self.qT[:, j, :], self.t_qT)
            self.proj_plain(wblk, f"fk{j}", self.kT[:, j, :], self.t_kT)
            self.proj_plain(wblk, f"fg{j}", self.sg[:, j, :], self.t_sg, silu=True)
        self.proj_v(wv, 0)
        s.dma("pool", self.wfb[:], wf, self.t_wfb, W=[self.t_wfb])
        b = 0
        for tb in range(NTB):
            for kc in range(NKC):
                s.op("pe", lambda e: e.matmul(self.B[b][:, tb * 2:(tb + 1) * 2], lhsT=self.hT[:, kc, tb * 128:(tb + 1) * 128],
                                              rhs=self.wfb[:, kc, :], start=(kc == 0), stop=(kc == NKC - 1)),
                     R=[self.t_wfb, self.t_hT], W=[self.t_B[b]])
        m2, t2 = self.sm2, self.t_sm2
        zb = m2[:, 0:32]
        z3 = zb.rearrange("p (t h) -> p t h", h=2)
        for h in range(2):
            s.op("dve", lambda e: e.tensor_scalar(out=z3[:, :, h], in0=self.B[b][:, 0:32].rearrange("p (t h) -> p t h", h=2)[:, :, h],
                                                  scalar1=self.fb[:, h:h + 1], scalar2=None, op0=ALU.add),
                 R=[self.t_B[b], self.t_par], W=[t2])
        s.op("act", lambda e: e.activation(out=zb, in_=zb, func=AF.Exp, scale=-1.0), R=[t2], W=[t2])
        s.op("act", lambda e: e.activation(out=zb, in_=zb, func=AF.Ln, bias=self.epsc[:, 1:2]), R=[t2, self.t_const], W=[t2])
        s.op("dve", lambda e: e.tensor_scalar(out=zb, in0=zb, scalar1=-1.0, scalar2=None, op0=ALU.mult), R=[t2], W=[t2])
        s.op("pe", lambda e: e.matmul(self.B[1][:, 0:32], lhsT=self.cf32[:, 128:256], rhs=zb, start=True, stop=True),
             R=[t2, self.t_const], W=[self.t_B[1]])
        s.op("pe", lambda e: e.matmul(self.B[2][:, 0:32], lhsT=self.cf32[:, 256:384], rhs=zb, start=True, stop=True),
             R=[t2, self.t_const], W=[self.t_B[2]])
        tot = m2[:, 32:64]
        incl = m2[:, 64:96]
        Ft = m2[:, 96:128]
        tot3 = tot.rearrange("p (t h) -> p t h", h=2)
        I3 = incl.rearrange("p (t h) -> p t h", h=2)
        F3 = Ft.rearrange("p (t h) -> p t h", h=2)
        s.op("dve", lambda e: e.tensor_copy(out=tot, in_=self.B[2][:, 0:32]), R=[self.t_B[2]], W=[t2])
        for h in range(2):
            s.op("dve", lambda e: e.tensor_tensor_scan(out=I3[:, :, h], data0=self.cf32[:, 256:256 + NTB], data1=tot3[:, :, h],
                                                       initial=0.0, op0=ALU.mult, op1=ALU.add),
                 R=[t2, self.t_const], W=[t2])
        s.op("dve", lambda e: e.tensor_tensor(out=Ft, in0=incl, in1=tot, op=ALU.subtract), R=[t2], W=[t2])
        s.op("dve", lambda e: e.tensor_tensor(out=Ft, in0=Ft, in1=self.B[1][:, 0:32], op=ALU.add), R=[t2, self.t_B[1]], W=[t2])
        biasT = m2[:, 128:256].rearrange("p (h q k) -> p h q k", h=2, q=4)
        for h in range(2):
            for qc in range(4):
                s.op("dve", lambda e: e.tensor_scalar(out=biasT[:, h, qc, :], in0=F3[:, :, h], scalar1=I3[:, 4 * qc + 3, h:h + 1],
                                                      scalar2=-1.0, op0=ALU.subtract, op1=ALU.mult),
                     R=[t2], W=[t2])
        units = []
        for j in range(2):
            def epi(qc, pair, j=j):
                c0, c1 = qc * 512, (qc + 1) * 512
                self.od_normalize(pair, self.ft[0], self.t_ft[0], self.ft[1], self.t_ft[1])
                s.op("pool", lambda e: e.tensor_tensor(out=self.mixT[:, j, c0:c1], in0=self.ft[1][:], in1=self.sg[:, j, c0:c1], op=ALU.mult),
                     R=[self.t_ft[1], self.t_sg], W=[self.t_mix[j]])
            job = self.mkjob(j, epi, biasT=biasT)
            units += [(job, qc) for qc in range(4)]
        self.attention(units)

    def diff(self, W, rope_ap):
        s = self.s
        wblk, wv = W["wblk"], W["wv"]
        for j in range(2):
            self.proj_rope(wblk, f"dq{j}", f"dqs{j}", self.qT[:, j, :], self.t_qT, rope_ap, 0)
            self.proj_rope(wblk, f"dk{j}", f"dks{j}", self.kT[:, j, :], self.t_kT, rope_ap, 0)
            self.proj_plain(wblk, f"dg{j}", self.sg[:, j, :], self.t_sg, silu=True)
        self.proj_v(wv, 1)
        sm, tsm = self.sm, self.t_sm
        l4 = self.lam[:, :].rearrange("p (a b c) -> p a b c", a=2, b=2)
        prod = self.sm3[:, 0:128].rearrange("p (a c) -> p a c", a=2)
        s.op("dve", lambda e: e.tensor_tensor(out=prod, in0=l4[:, :, 0, :], in1=l4[:, :, 1, :], op=ALU.mult), R=[self.t_par], W=[self.t_sm3])
        s.op("dve", lambda e: e.tensor_reduce(out=sm[:, 8:10], in_=prod, axis=AX.X, op=ALU.add), R=[self.t_sm3], W=[tsm])
        s.op("act", lambda e: e.activation(out=sm[:, 10:12], in_=sm[:, 8:10], func=AF.Exp), R=[tsm], W=[tsm])
        s.op("dve", lambda e: e.tensor_tensor(out=sm[:, 12:13], in0=sm[:, 11:12], in1=sm[:, 10:11], op=ALU.subtract), R=[tsm], W=[tsm])
        s.op("dve", lambda e: e.tensor_tensor(out=sm[:, 12:13], in0=sm[:, 12:13], in1=self.lsc[:, 0:1], op=ALU.subtract), R=[tsm, self.t_par], W=[tsm])
        s.op("dve", lambda e: e.tensor_tensor(out=sm[:, 13:14], in0=self.dng[:, 0:1], in1=self.lsc[:, 1:2], op=ALU.mult), R=[tsm, self.t_par], W=[tsm])
        units = []
        for j in range(2):
            def epi1(qc, pair, j=j):
                self.od_normalize(pair, self.ft[0], self.t_ft[0], self.ft[1], self.t_ft[1])

            def epi2(qc, pair, j=j):
                self.od_normalize(pair, self.ft[0], self.t_ft[0], self.ft[2], self.t_ft[2])
                f3, t3 = self.ft[3], self.t_ft[3]
                s.op("dve", lambda e: e.scalar_tensor_tensor(out=f3[:], in0=self.ft[2][:], scalar=sm[:, 12:13], in1=self.ft[1][:],
                                                             op0=ALU.mult, op1=ALU.add),
                     R=[self.t_ft[2], self.t_ft[1], tsm], W=[t3])
                self.normalize_gate(f3, t3, j, qc, sm[:, 13:14], 2 + j, nb=pair[1], scr=(0, 2))
            j1 = self.mkjob(j, epi1, rows=(0, 64), scale=64 ** -0.5)
            j2 = self.mkjob(j, epi2, rows=(64, 128), scale=64 ** -0.5)
            for qc in range(4):
                units += [(j1, qc), (j2, qc)]
        self.attention(units)

    def moba(self, W, rope_ap):
        s = self.s
        wblk, wv = W["wblk"], W["wv"]
        for j in range(2):
            self.proj_rope(wblk, f"mq{j}", f"mqs{j}", self.qT[:, j, :], self.t_qT, rope_ap, 1)
            self.proj_rope(wblk, f"mk{j}", f"mks{j}", self.kT[:, j, :], self.t_kT, rope_ap, 1)
            self.proj_plain(wblk, f"mg{j}", self.sg[:, j, :], self.t_sg, silu=True)
        self.proj_v(wv, 2)
        m2, t2 = self.sm2, self.t_sm2
        m3, t3 = self.sm3, self.t_sm3
        for j in range(2):
            s.op("dve", lambda e: e.tensor_reduce(out=m3[:, 0:8], in_=self.kT[:, j, :].rearrange("p (n l) -> p n l", l=256), axis=AX.X, op=ALU.add),
                 R=[self.t_kT], W=[t3])
            s.op("dve", lambda e: e.tensor_scalar(out=self.kmT[:, :], in0=m3[:, 0:8], scalar1=1.0 / 256, scalar2=None, op0=ALU.mult),
                 R=[t3], W=[self.t_kmT])
            gb = 4
            for qb in range(8, 16):
                s.op("pe", lambda e: e.matmul(self.B[gb][:, (qb - 8) * 8:(qb - 7) * 8], lhsT=self.qT[:, j, qb * 128:(qb + 1) * 128],
                                              rhs=self.kmT[:, :], start=True, stop=True),
                     R=[self.t_qT, self.t_kmT], W=[self.t_B[gb]])
            gsb = m2[:, 0:64].rearrange("p (s n) -> p s n", n=8)
            gps = self.B[gb][:, 0:64].rearrange("p (s n) -> p s n", n=8)
            s.op("dve", lambda e: e.memset(m2[:, 0:64], -1e30), W=[t2])
            for own in range(4, 8):
                sl = slice(2 * own - 8, 2 * own - 6)
                s.op("dve", lambda e: e.tensor_copy(out=gsb[:, sl, 0:own], in_=gps[:, sl, 0:own]), R=[self.t_B[gb]], W=[t2])
            top = m2[:, 64:128].rearrange("p (s n) -> p s n", n=8)
            for sl in range(8):
                s.op("dve", lambda e: e.max(out=top[:, sl, :], in_=gsb[:, sl, :]), R=[t2], W=[t2])
            for sl in range(8):
                s.op("dve", lambda e: e.tensor_scalar(out=self.nsel[:, sl, :], in0=gsb[:, sl, :], scalar1=top[:, sl, 2:3], scalar2=NEG,
                                                      op0=ALU.is_lt, op1=ALU.mult),
                     R=[t2], W=[self.t_nsel])
            for sl in range(8):
                s.op("dve", lambda e: e.tensor_copy(out=self.negB[:, sl, :, :],
                                                     in_=self.nsel[:, sl, :].rearrange("p (n o) -> p n o", o=1).to_broadcast([128, 8, 128])),
                     R=[self.t_nsel], W=[self.t_scr])

            def masks(kb, qc):
                out = []
                j0 = max(0, kb - 4 * qc)
                for jj in range(j0, 4):
                    qb = 4 * qc + jj
                    own = qb // 2
                    if own >= 4 and kb // 2 < own:
                        out.append((jj, self.negB[:, qb - 8, kb // 2, :], self.t_scr))
                return out

            def epi(qc, pair, j=j):
                c0, c1 = qc * 512, (qc + 1) * 512
                self.od_normalize(pair, self.ft[0], self.t_ft[0], self.ft[1], self.t_ft[1])
                s.op("pool", lambda e: e.tensor_tensor(out=self.mixT[:, 4 + j, c0:c1], in0=self.ft[1][:], in1=self.sg[:, j, c0:c1], op=ALU.mult),
                     R=[self.t_ft[1], self.t_sg], W=[self.t_mix[4 + j]])
            job = self.mkjob(j, epi, masks=masks)
            self.attention([(job, qc) for qc in range(4)])

    def gla(self, W):
        s = self.s
        wblk, wv = W["wblk"], W["wv"]
        sm, tsm = self.sm, self.t_sm
        m3, t3 = self.sm3, self.t_sm3
        tscr = self.t_scr
        qg, kend, kgz = self.qT[:, 0, :], self.qT[:, 1, :], self.kT
        self.proj_plain(wblk, "ga", self.kT[:, 1, :], self.t_aT, M=16, eng_alt=False)
        s.op("dve", lambda e: e.tensor_scalar(out=sm[:, 14:15], in0=self.gba[:, 0:1], scalar1=-1.0, scalar2=None, op0=ALU.mult),
             R=[self.t_par], W=[tsm])
        for tc in range(4):
            b = self.next_bank()
            c = slice(tc * 512, (tc + 1) * 512)
            s.op("pe", lambda e: e.matmul(self.B[b][:], lhsT=self.wa2[0:16, :], rhs=self.aT[0:16, c], start=True, stop=True),
                 R=[self.t_wa2, self.t_aT], W=[self.t_B[b]])
            s.op("act", lambda e: e.activation(out=self.gcs[:, c], in_=self.B[b][:], func=AF.Exp, bias=sm[:, 14:15], scale=-1.0),
                 R=[self.t_B[b], tsm], W=[tscr])
        s.op("act", lambda e: e.activation(out=self.gcs[:, :], in_=self.gcs[:, :], func=AF.Ln, bias=self.epsc[:, 1:2]), R=[tscr, self.t_const], W=[tscr])
        s.op("dve", lambda e: e.tensor_scalar(out=self.gcs[:, :], in0=self.gcs[:, :], scalar1=-1.0 / 16.0, scalar2=None, op0=ALU.mult),
             R=[tscr], W=[tscr])
        for c in range(NTB):
            cs = slice(c * 128, (c + 1) * 128)
            s.op("dve", lambda e: e.tensor_tensor_scan(out=self.gcs[:, cs], data0=self.cf32[:, 256:384], data1=self.gcs[:, cs],
                                                       initial=0.0, op0=ALU.mult, op1=ALU.add),
                 R=[tscr, self.t_const], W=[tscr])
        s.op("act", lambda e: e.activation(out=self.eg[:, :], in_=self.gcs[:, :], func=AF.Exp), R=[tscr], W=[tscr])
        s.op("act", lambda e: e.activation(out=self.eng[:, :], in_=self.gcs[:, :], func=AF.Exp, scale=-1.0), R=[tscr], W=[tscr])
        egl = m3[:, 0:16]
        s.op("act", lambda e: e.activation(out=egl, in_=self.gcs[:, :].rearrange("p (c t) -> p c t", t=128)[:, :, 127], func=AF.Exp),
             R=[tscr], W=[t3])
        s.op("dve", lambda e: e.tensor_tensor(out=self.eend[:, :].rearrange("p (c t) -> p c t", t=128),
                                              in0=self.eng[:, :].rearrange("p (c t) -> p c t", t=128),
                                              in1=egl.rearrange("p (c o) -> p c o", o=1).to_broadcast([128, NTB, 128]), op=ALU.mult),
             R=[tscr, t3], W=[tscr])
        w, tw = self.load_wblk(wblk, "gq")
        for tc in range(4):
            b = self.next_bank()
            c = slice(tc * 512, (tc + 1) * 512)
            self.proj_chunk(w, tw, tc, b)
            s.op("dve", lambda e: e.scalar_tensor_tensor(out=qg[:, c], in0=self.B[b][:], scalar=0.125, in1=self.eg[:, c], op0=ALU.mult, op1=ALU.mult),
                 R=[self.t_B[b], tscr], W=[self.t_qT])
        w, tw = self.load_wblk(wblk, "gk")
        s.op("dve", lambda e: e.memset(kgz[:, :, :], 0.0), W=[self.t_kT])
        for tc in range(4):
            b = self.next_bank()
            c = slice(tc * 512, (tc + 1) * 512)
            self.proj_chunk(w, tw, tc, b)
            for j in range(2):
                r = slice(j * 64, (j + 1) * 64)
                s.op("dve", lambda e: e.tensor_tensor(out=kgz[r, j, c], in0=self.B[b][r, :], in1=self.eng[r, c], op=ALU.mult),
                     R=[self.t_B[b], tscr], W=[self.t_kT])
            s.op("dve", lambda e: e.tensor_tensor(out=kend[:, c], in0=self.B[b][:], in1=self.eend[:, c], op=ALU.mult), R=[self.t_B[b], tscr], W=[self.t_qT])
        for half in range(2):
            for k in range(8):
                c = half * 8 + k
                s.op("pe", lambda e: e.transpose(out=self.T[:, k * 128:(k + 1) * 128], in_=kend[:, c * 128:(c + 1) * 128], identity=self.ident[:]),
                     R=[self.t_qT, self.t_const], W=[self.t_T])
            s.op("act", lambda e: e.copy(out=self.ktok[:, half * 8:(half + 1) * 8, :], in_=self.T[:, :].rearrange("p (k t) -> p k t", t=128)),
                 R=[self.t_T], W=[self.t_ktok])
        for j in range(2):
            self.proj_plain(wblk, f"gg{j}", self.sg[:, j, :], self.t_sg, silu=True)
        self.proj_v(wv, 3)
        if getattr(self, "pre_b", None) is not None:
            self.pre_b()
        s.op("dve", lambda e: e.memset(self.S32[:], 0.0), W=[self.t_S32])
        s.op("dve", lambda e: e.memset(self.S16[:, :, :], 0.0), W=[self.t_S16])
        tri2 = self.tri[:, :].rearrange("p (o t) -> p o t", o=1).to_broadcast([128, 2, 128])
        ats, t_ats = [self.at, self.at2], [self.t_at, self.t_at2]
        S16s, t_S16s = [self.S16, self.S16b], [self.t_S16, self.t_S16b]
        s.op("dve", lambda e: e.memset(self.S16b[:, :, :], 0.0), W=[self.t_S16b])
        ubs = (6, 4)
        ab = 5

        def emit_AT(c):
            cs = slice(c * 128, (c + 1) * 128)
            for j in range(2):
                s.op("pe", lambda e: e.matmul(self.B[ab][:, j * 128:(j + 1) * 128], lhsT=kgz[:, j, cs], rhs=qg[:, cs], start=True, stop=True),
                     R=[self.t_qT, self.t_kT], W=[self.t_B[ab]])
            s.op("dve", lambda e: e.tensor_tensor(out=ats[c % 2][:, :, :], in0=self.B[ab][:, 0:256].rearrange("p (j t) -> p j t", j=2), in1=tri2, op=ALU.mult),
                 R=[self.t_B[ab], self.t_const], W=[t_ats[c % 2]])

        def emit_U(c):
            ub = ubs[c % 2]
            s.op("pe", lambda e: e.matmul(self.B[ub][:, 0:256], lhsT=self.ktok[:, c, :], rhs=self.v[:, c, :], start=True, stop=True),
                 R=[self.t_ktok, self.t_v], W=[self.t_B[ub]])

        def emit_o(c, obs, cc):
            cs = slice(c * 128, (c + 1) * 128)
            for j in range(2):
                ob = obs[j]
                s.op("pe", lambda e: e.matmul(self.B[ob][:, cc * 128:(cc + 1) * 128], lhsT=self.v[:, c, j * 128:(j + 1) * 128], rhs=ats[c % 2][:, j, :],
                                              start=True, stop=False),
                     R=[self.t_v, t_ats[c % 2]], W=[self.t_B[ob]])
                s.op("pe", lambda e: e.matmul(self.B[ob][:, cc * 128:(cc + 1) * 128], lhsT=S16s[c % 2][:, j, :], rhs=qg[:, cs], start=False, stop=True),
                     R=[t_S16s[c % 2], self.t_qT], W=[self.t_B[ob]])

        def emit_upd(c):
            ub = ubs[c % 2]
            for j in range(2):
                r = slice(j * 64, (j + 1) * 64)
                s.op("dve", lambda e: e.scalar_tensor_tensor(out=self.S32[r, :], in0=self.S32[r, :], scalar=egl[r, c:c + 1],
                                                             in1=self.B[ub][r, j * 128:(j + 1) * 128], op0=ALU.mult, op1=ALU.add),
                     R=[self.t_S32, t3, self.t_B[ub]], W=[self.t_S32])
            nxt = (c + 1) % 2
            for j in range(2):
                r = slice(j * 64, (j + 1) * 64)
                s.op("act", lambda e: e.copy(out=S16s[nxt][r, j, :], in_=self.S32[r, :]), R=[self.t_S32], W=[t_S16s[nxt]])

        emit_AT(0)
        emit_U(0)
        for tg in range(4):
            obs = (0, 1) if tg % 2 == 0 else (2, 3)
            for cc in range(4):
                c = tg * 4 + cc
                if c + 1 < NTB:
                    emit_AT(c + 1)
                    emit_U(c + 1)
                emit_o(c, obs, cc)
                if c + 1 < NTB:
                    emit_upd(c)
            for j in range(2):
                f3, tf3 = self.ft[3], self.t_ft[3]
                s.op("act", lambda e: e.copy(out=f3[:], in_=self.B[obs[j]][:]), R=[self.t_B[obs[j]]], W=[tf3])
                self.normalize_gate(f3, tf3, j, tg, self.gng[:, 0:1], 6 + j, nb=obs[j], scr=(0, 1))

    def program_a(self, x_ap, W, rope_ap, mix_out_ap, groups=("fox", "diff", "moba", "gla"), t_x=None, mixd=None, t_mixd=None):
        s = self.s
        self.load_params(W)
        self.norm_phase(x_ap, W["gbc"], t_x)
        hook = mixd if callable(mixd) else (lambda g: None)
        if "fox" in groups:
            self.fox(W)
            hook(0)
        if "diff" in groups:
            self.diff(W, rope_ap)
            hook(1)
        if "moba" in groups:
            self.moba(W, rope_ap)
            hook(2)
        if "gla" in groups:
            self.gla(W)
            hook(3)
        if callable(mixd):
            return
        for ch in range(8):
            g, j = ch // 2, ch % 2
            if ("fox", "diff", "moba", "gla")[g] not in groups:
                continue
            if mixd is not None:
                s.dma("sp", mixd[ch * 128:(ch + 1) * 128, :], self.mixT[:, ch, :], self.t_mix[ch], R=[self.t_mix[ch]], W=[t_mixd])
                continue
            s.dma("pool", mix_out_ap[ch], self.mixT[:, ch, :], self.t_mix[ch], R=[self.t_mix[ch]], W=[self.t_out])


def _dram(nc, name, shape, kind="ExternalInput", dt=F32):
    return nc.dram_tensor(name, list(shape), dt, kind=kind).ap()


A_INPUTS = (("x", [S, D]), ("wblk", [NBLK, 128, NKC, 128]), ("wv", [4, 128, NKC, 256]), ("wf", [128, NKC, 2]),
            ("gbc", [128, D]), ("fb", [128, 2]), ("lam", [128, 256]), ("lsc", [128, 8]), ("dng", [128, 1]),
            ("gng", [128, 1]), ("gba", [128, 1]), ("wa2", [16, 128]), ("rope", [4, 128, S]), ("consts", [128, 384]))


def build_a(groups=("fox", "diff", "moba", "gla")):
    nc = bass.Bass("TRN2", target_bir_lowering=False)
    aps = {n: _dram(nc, n, shp) for n, shp in A_INPUTS}
    mix = _dram(nc, "mix", [8, 128, S], kind="ExternalOutput")
    p = Prog(nc)
    p.load_consts(aps["consts"])
    p.program_a(aps["x"], aps, aps["rope"], mix, groups)
    p.s.finish([p.t_out])
    return nc, p


def layer_inputs_a(inputs, l, b, hh, rope, consts):
    wblk, wv, wf = pack_layer_weights(inputs["w_in"][l], hh)
    lambda_init = 0.8 - 0.6 * math.exp(-0.3 * l)
    lsc = np.zeros((128, 8), np.float32)
    lsc[:, 0] = lambda_init
    lsc[:, 1] = 1.0 - lambda_init
    bc = lambda v, n: np.ascontiguousarray(np.broadcast_to(np.asarray(v, np.float32).reshape(1, n), (128, n)))
    return {
        "x": np.ascontiguousarray(inputs["x"][b]),
        "wblk": wblk, "wv": wv, "wf": wf,
        "gbc": bc(inputs["norm_g"][l], D),
        "fb": bc(inputs["fox_fb"][l][2 * hh:2 * hh + 2], 2),
        "lam": bc(inputs["diff_lam"][l].reshape(-1), 256),
        "lsc": lsc,
        "dng": np.ascontiguousarray(inputs["diff_norm_g"][l].reshape(128, 1)),
        "gng": np.ascontiguousarray(inputs["gla_norm_g"][l].reshape(128, 1)),
        "gba": np.ascontiguousarray(inputs["gla_ba"][l][hh * 128:(hh + 1) * 128].reshape(128, 1)),
        "wa2": np.ascontiguousarray(inputs["gla_wa2"][l][:, hh * 128:(hh + 1) * 128]),
        "rope": rope, "consts": consts,
    }


def emit_b(p, x_ap, mixf_ap, wout_ap, T, xo_ap=None, fg_ap=None, y_ap=None):
    s, nc = p.s, p.nc
    wo = nc.alloc_sbuf_tensor_at("wo", [128, 16, D], BF16, offset=p.hT_off)
    mf = nc.alloc_sbuf_tensor_at("mf", [128, 16, T], BF16, offset=p.mix_off)
    yt = nc.alloc_sbuf_tensor_at("yt", [128, D], F32, offset=p.g0 + 16384)
    t_wo = s.toks("wo", 16)
    t_mf = s.toks("mf", 16)
    for c in range(16):
        s.dma("pool", wo[:, c, :], wout_ap[c], t_wo[c], R=[], W=[t_wo[c], p.t_hT])
        s.dma("pool", mf[:, c, :], mixf_ap[c], t_mf[c], R=[], W=[t_mf[c]] + ([p.t_mix[c // 2]] if False else []))
    if fg_ap is not None:
        s.dma("sp", p.gbc[:], fg_ap, p.t_gbc, W=[p.t_gbc])
    for tb in range(T // 128):
        i = tb % 2
        xt, txt = p.xt[i], p.t_xt[i]
        s.dma("sp", xt[:], x_ap[tb * 128:(tb + 1) * 128, :], txt, W=[txt])
        for fc in range(4):
            b = p.next_bank()
            for c in range(16):
                s.op("pe", lambda e: e.matmul(p.B[b][:], lhsT=mf[:, c, tb * 128:(tb + 1) * 128], rhs=wo[:, c, fc * 512:(fc + 1) * 512],
                                              start=(c == 0), stop=(c == 15)),
                     R=[t_wo[c], t_mf[c]], W=[p.t_B[b]])
            s.op("dve", lambda e: e.tensor_tensor(out=xt[:, fc * 512:(fc + 1) * 512], in0=xt[:, fc * 512:(fc + 1) * 512], in1=p.B[b][:], op=ALU.add),
                 R=[p.t_B[b], txt], W=[txt])
        if xo_ap is not None:
            s.dma("sp", xo_ap[tb * 128:(tb + 1) * 128, :], xt[:], txt, R=[txt], W=[p.t_out])
        if y_ap is not None:
            ss = p.sm[:, 0:1]
            s.op("act", lambda e: e.activation(out=yt[:], in_=xt[:], func=AF.Square, accum_out=ss), R=[txt], W=[p.t_v, p.t_sm])
            s.op("act", lambda e: e.activation(out=p.sm[:, 1:2], in_=ss, func=AF.Sqrt, bias=p.epsc[:, 0:1], scale=1.0 / D),
                 R=[p.t_sm, p.t_const], W=[p.t_sm])
            s.op("dve", lambda e: e.reciprocal(out=p.sm[:, 2:3], in_=p.sm[:, 1:2]), R=[p.t_sm], W=[p.t_sm])
            s.op("dve", lambda e: e.scalar_tensor_tensor(out=yt[:], in0=xt[:], scalar=p.sm[:, 2:3], in1=p.gbc[:], op0=ALU.mult, op1=ALU.mult),
                 R=[txt, p.t_sm, p.t_gbc], W=[p.t_v])
            s.dma("sp", y_ap[tb * 128:(tb + 1) * 128, :], yt[:], p.t_v, R=[p.t_v], W=[p.t_out])


def build_b(T, final):
    nc = bass.Bass("TRN2", target_bir_lowering=False)
    x = _dram(nc, "x", [T, D])
    mixf = _dram(nc, "mixf", [16, 128, T])
    wout = _dram(nc, "wout", [16, 128, D])
    consts = _dram(nc, "consts", [128, 384])
    p = Prog(nc)
    p.load_consts(consts)
    if final:
        fg = _dram(nc, "fg", [128, D])
        y = _dram(nc, "y", [T, D], kind="ExternalOutput")
        emit_b(p, x, mixf, wout, T, fg_ap=fg, y_ap=y)
    else:
        xo = _dram(nc, "xo", [T, D], kind="ExternalOutput")
        emit_b(p, x, mixf, wout, T, xo_ap=xo)
    p.s.finish([p.t_out])
    return nc, p


def wout_perm(w_out_l):
    rows = []
    for r in range(2):
        for g in range(4):
            for j in range(2):
                rows.append(g * 512 + (2 * r + j) * 128 + np.arange(128))
    return np.ascontiguousarray(w_out_l[np.concatenate(rows)].reshape(16, 128, D))


_CACHE = {}


def kernel(x, norm_g, w_in, fox_fb, diff_lam, diff_norm_g, gla_wa2, gla_ba, gla_norm_g, w_out, final_norm_g):
    inputs = dict(x=np.asarray(x, np.float32), norm_g=np.asarray(norm_g, np.float32), w_in=np.asarray(w_in, np.float32),
                  fox_fb=np.asarray(fox_fb, np.float32), diff_lam=np.asarray(diff_lam, np.float32),
                  diff_norm_g=np.asarray(diff_norm_g, np.float32), gla_wa2=np.asarray(gla_wa2, np.float32),
                  gla_ba=np.asarray(gla_ba, np.float32), gla_norm_g=np.asarray(gla_norm_g, np.float32))
    w_out = np.asarray(w_out, np.float32)
    rope, consts = rope_tables(), const_tables()
    if "a" not in _CACHE:
        _CACHE["a"] = build_a()[0]
        _CACHE["b"] = build_b(S // 2, False)[0]
        _CACHE["bf"] = build_b(S // 2, True)[0]
    nca, ncb, ncbf = _CACHE["a"], _CACHE["b"], _CACHE["bf"]
    B_ = inputs["x"].shape[0]
    xcur = inputs["x"]
    fg = np.ascontiguousarray(np.broadcast_to(np.asarray(final_norm_g, np.float32).reshape(1, D), (128, D)))
    cores = list(range(8))
    for l in range(DEPTH):
        cur = dict(inputs)
        cur["x"] = xcur
        per_hh = [layer_inputs_a(cur, l, 0, hh, rope, consts) for hh in range(2)]
        maps = []
        for c in cores:
            b, hh = c // 2, c % 2
            m = dict(per_hh[hh])
            m["x"] = np.ascontiguousarray(xcur[b])
            maps.append(m)
        res = run_bass_kernel_spmd(nca, maps, core_ids=cores)
        wo = wout_perm(w_out[l])
        maps = []
        for c in cores:
            b, th = c // 2, c % 2
            mixf = np.concatenate([res.results[2 * b]["mix"], res.results[2 * b + 1]["mix"]], axis=0)
            m = {"x": np.ascontiguousarray(xcur[b, th * 1024:(th + 1) * 1024]),
                 "mixf": np.ascontiguousarray(mixf[:, :, th * 1024:(th + 1) * 1024]),
                 "wout": wo, "consts": consts}
            if l == DEPTH - 1:
                m["fg"] = fg
            maps.append(m)
        if l < DEPTH - 1:
            resb = run_bass_kernel_spmd(ncb, maps, core_ids=cores)
            xcur = np.stack([np.concatenate([resb.results[2 * b]["xo"], resb.results[2 * b + 1]["xo"]], axis=0) for b in range(B_)])
        else:
            resb = run_bass_kernel_spmd(ncbf, maps, core_ids=cores)
            out = np.stack([np.concatenate([resb.results[2 * b]["y"], resb.results[2 * b + 1]["y"]], axis=0) for b in range(B_)])
    return out.astype(np.float32)


F_INPUTS = (("x", [S, D]), ("wblk", [DEPTH, NBLK, 128, NKC, 128]), ("wv", [DEPTH, 4, 128, NKC, 256]), ("wf", [DEPTH, 128, NKC, 2]),
            ("gbc", [DEPTH, 128, D]), ("fb", [DEPTH, 128, 2]), ("lam", [DEPTH, 128, 256]), ("lsc", [DEPTH, 128, 8]),
            ("dng", [DEPTH, 128, 1]), ("gng", [DEPTH, 128, 1]), ("gba", [DEPTH, 128, 1]), ("wa2", [DEPTH, 16, 128]),
            ("wout", [DEPTH, 16, 128, D]), ("fg", [128, D]), ("rope", [4, 128, S]), ("consts", [128, 384]))
PAIRS = [[0, 1], [2, 3], [4, 5], [6, 7]]


def emit_b_fused(p, l, x_ap, t_xin, mixfull, t_mixfull, wout_ap, xo_ap, t_xo, y_ap):
    s, nc = p.s, p.nc
    T = 1024
    wo = p.wo
    mf = nc.alloc_sbuf_tensor_at(f"mf{l}", [128, 16, T], BF16, offset=p.mix_off)
    yt = nc.alloc_sbuf_tensor_at(f"yt{l}", [128, D], F32, offset=p.g0 + 16384)
    if not hasattr(p, "t_wo"):
        p.t_wo = s.toks("wo", 16)
        p.t_mf = s.toks("mf", 16)
    t_wo, t_mf = p.t_wo, p.t_mf
    alias = [p.t_hT] + p.t_mix + t_wo + t_mf
    s.op("dve", lambda e: e.memset(p.sm[:, 20:21], 0.0), W=p.t_mix + t_mf + [p.t_sm])
    for half in range(2):
        t0 = half * T
        for c in range(16):
            r_, ch_ = c // 8, c % 8
            src = mixfull[ch_ // 4].ap()[r_ * 512 + (ch_ % 4) * 128:r_ * 512 + (ch_ % 4 + 1) * 128, t0:t0 + T]
            s.dma("sp", mf[:, c, :], src, t_mf[c], R=[t_mixfull[ch_ // 4]], W=[t_mf[c]])
        for tb in range(T // 128):
            i = tb % 2
            xt, txt = p.xt[i], p.t_xt[i]
            r0 = t0 + tb * 128
            s.dma("sp", xt[:], x_ap[r0:r0 + 128, :], txt, R=([t_xin] if t_xin is not None else []), W=[txt])
            for fc in range(4):
                b = p.next_bank()
                for c in range(16):
                    s.op("pe", lambda e: e.matmul(p.B[b][:], lhsT=mf[:, c, tb * 128:(tb + 1) * 128], rhs=wo[:, c, fc * 512:(fc + 1) * 512],
                                                  start=(c == 0), stop=(c == 15)),
                         R=[t_wo[c], t_mf[c]], W=[p.t_B[b]])
                s.op("dve", lambda e: e.tensor_tensor(out=xt[:, fc * 512:(fc + 1) * 512], in0=xt[:, fc * 512:(fc + 1) * 512], in1=p.B[b][:], op=ALU.add),
                     R=[p.t_B[b], txt], W=[txt])
            if xo_ap is not None:
                s.dma("sp", xo_ap[r0:r0 + 128, :], xt[:], txt, R=[txt], W=[t_xo])
            if y_ap is not None:
                ss = p.sm[:, 0:1]
                s.op("act", lambda e: e.activation(out=yt[:], in_=xt[:], func=AF.Square, accum_out=ss), R=[txt], W=[p.t_v, p.t_sm])
                s.op("act", lambda e: e.activation(out=p.sm[:, 1:2], in_=ss, func=AF.Sqrt, bias=p.epsc[:, 0:1], scale=1.0 / D),
                     R=[p.t_sm, p.t_const], W=[p.t_sm])
                s.op("dve", lambda e: e.reciprocal(out=p.sm[:, 2:3], in_=p.sm[:, 1:2]), R=[p.t_sm], W=[p.t_sm])
                s.op("dve", lambda e: e.scalar_tensor_tensor(out=yt[:], in0=xt[:], scalar=p.sm[:, 2:3], in1=p.gbc[:], op0=ALU.mult, op1=ALU.mult),
                     R=[txt, p.t_sm, p.t_gbc], W=[p.t_v])
                s.dma("sp", y_ap[r0:r0 + 128, :], yt[:], p.t_v, R=[p.t_v], W=[p.t_out])
    s.op("dve", lambda e: e.memset(p.sm[:, 20:21], 0.0), W=alias + [p.t_sm])


def build_fused(depth=DEPTH):
    nc = bass.Bass("TRN2", target_bir_lowering=False)
    aps = {n: _dram(nc, n, ([depth] + list(shp[1:])) if (len(shp) > 2 and shp[0] == DEPTH and n not in ("rope",)) else shp) for n, shp in F_INPUTS}
    y = _dram(nc, "y", [S, D], kind="ExternalOutput")
    xs = [nc.dram_tensor(f"xs{i}", [S, D], F32).ap() for i in range(2)]
    mo = [nc.dram_tensor(f"mixown{i}", [4 * 128, S], BF16) for i in range(2)]
    mfu = [nc.dram_tensor(f"mixfull{i}", [2 * 4 * 128, S], BF16) for i in range(2)]
    p = Prog(nc)
    s = p.s
    t_xs = s.toks("xs", 2)
    t_mo, t_mfull = s.toks("mixown", 2), s.toks("mixfull", 2)
    p.wo = nc.alloc_sbuf_tensor_at("wo", [128, 16, D], BF16, offset=p.hT_off)
    p.t_wo = s.toks("wo", 16)
    p.t_mf = s.toks("mf", 16)
    p.load_consts(aps["consts"])
    x_in, t_xin = aps["x"], None
    for l in range(depth):
        W = {k: aps[k][l] for k in ("wblk", "wv", "wf", "gbc", "fb", "lam", "lsc", "dng", "gng", "gba", "wa2")}

        def hook(g):
            for ch in (2 * g, 2 * g + 1):
                hs = ch // 4
                s.dma("sp", mo[hs].ap()[(ch % 4) * 128:(ch % 4 + 1) * 128, :], p.mixT[:, ch, :], p.t_mix[ch], R=[p.t_mix[ch]], W=[t_mo[hs]])
            if g % 2 == 1:
                hs = g // 2
                s.coll(lambda e: e.collective_compute("AllGather", ALU.bypass, replica_groups=PAIRS,
                                                      ins=[mo[hs].ap().opt()], outs=[mfu[hs].ap().opt()]),
                       R=[t_mo[hs]], W=[t_mfull[hs]])
        def pre_b(l=l):
            for c in range(16):
                s.dma("pool", p.wo[:, c, :], aps["wout"][l][c], p.t_wo[c], W=[p.t_wo[c]] + ([p.t_hT] if c == 0 else []))
        p.pre_b = pre_b
        p.program_a(x_in, W, aps["rope"], None, t_x=t_xin, mixd=hook)
        last = (l == depth - 1)
        if last:
            s.dma("sp", p.gbc[:], aps["fg"], p.t_gbc, W=[p.t_gbc])
        k = l % 2
        emit_b_fused(p, l, x_in, t_xin, mfu, t_mfull, aps["wout"][l],
                     None if last else xs[k], None if last else t_xs[k], y if last else None)
        x_in, t_xin = xs[k], t_xs[k]
    p.s.finish([p.t_out])
    return nc, p


def fused_inputs(inputs, w_out, final_norm_g, b, hh, rope, consts, packs):
    m = {k: packs[hh][k] for k in packs[hh]}
    m["x"] = np.ascontiguousarray(inputs["x"][b])
    m["rope"] = rope
    m["consts"] = consts
    return m


def kernel(x, norm_g, w_in, fox_fb, diff_lam, diff_norm_g, gla_wa2, gla_ba, gla_norm_g, w_out, final_norm_g):
    inputs = dict(x=np.asarray(x, np.float32), norm_g=np.asarray(norm_g, np.float32), w_in=np.asarray(w_in, np.float32),
                  fox_fb=np.asarray(fox_fb, np.float32), diff_lam=np.asarray(diff_lam, np.float32),
                  diff_norm_g=np.asarray(diff_norm_g, np.float32), gla_wa2=np.asarray(gla_wa2, np.float32),
                  gla_ba=np.asarray(gla_ba, np.float32), gla_norm_g=np.asarray(gla_norm_g, np.float32))
    w_out = np.asarray(w_out, np.float32)
    rope, consts = rope_tables(), const_tables()
    if "f" not in _CACHE:
        _CACHE["f"] = build_fused()[0]
    nc = _CACHE["f"]
    fg = np.ascontiguousarray(np.broadcast_to(np.asarray(final_norm_g, np.float32).reshape(1, D), (128, D)))
    wo_all = np.stack([wout_perm(w_out[l]) for l in range(DEPTH)])
    packs = []
    for hh in range(2):
        per_l = [layer_inputs_a(inputs, l, 0, hh, rope, consts) for l in range(DEPTH)]
        pk = {k: np.stack([per_l[l][k] for l in range(DEPTH)]) for k in ("wblk", "wv", "wf", "gbc", "fb", "lam", "lsc", "dng", "gng", "gba", "wa2")}
        pk["wout"] = wo_all
        pk["fg"] = fg
        packs.append(pk)
    cores = list(range(8))
    maps = [fused_inputs(inputs, w_out, final_norm_g, c // 2, c % 2, rope, consts, packs) for c in cores]
    res = run_bass_kernel_spmd(nc, maps, core_ids=cores)
    out = np.stack([res.results[2 * b]["y"] for b in range(inputs["x"].shape[0])])
    return out.astype(np.float32)


HD = D // 2
F2_INPUTS = (("x", [S, D]), ("xh", [S, HD]), ("wblk", [DEPTH, NBLK, 128, NKC, 128]), ("wv", [DEPTH, 4, 128, NKC, 256]), ("wf", [DEPTH, 128, NKC, 2]),
             ("gbc", [DEPTH, 128, D]), ("fb", [DEPTH, 128, 2]), ("lam", [DEPTH, 128, 256]), ("lsc", [DEPTH, 128, 8]),
             ("dng", [DEPTH, 128, 1]), ("gng", [DEPTH, 128, 1]), ("gba", [DEPTH, 128, 1]), ("wa2", [DEPTH, 16, 128]),
             ("wout", [DEPTH, 16, 128, HD]), ("fg", [128, D]), ("rope", [4, 128, S]), ("consts", [128, 384]))


def emit_b_fs(p, l, xin_piece, t_xin_piece, mixfull, t_mixfull, xop, t_xop, xg, t_xg):
    s, nc = p.s, p.nc
    T = 1024
    wo = p.wo
    mfs = [nc.alloc_sbuf_tensor_at(f"mfa{l}", [128, 16, T], BF16, offset=p.mix_off),
           nc.alloc_sbuf_tensor_at(f"mfb{l}", [128, 16, T], BF16, offset=p.hT_off + 32768)]
    if not hasattr(p, "t_mfb"):
        p.t_mfb = s.toks("mfb", 16)
    t_wo = p.t_wo
    t_mfs = [p.t_mf, p.t_mfb]
    alias = [p.t_hT] + p.t_mix + t_wo + p.t_mf + p.t_mfb
    s.op("dve", lambda e: e.memset(p.sm[:, 20:21], 0.0), W=p.t_mix + p.t_mf + [p.t_sm])
    for half in range(2):
        t0 = half * T
        for c in range(16):
            r_, ch_ = c // 8, c % 8
            src = mixfull[ch_ // 4].ap()[r_ * 512 + (ch_ % 4) * 128:r_ * 512 + (ch_ % 4 + 1) * 128, t0:t0 + T]
            extra = [p.t_hT] if (half == 1 and c == 0) else []
            s.dma("sp", mfs[half][:, c, :], src, t_mfs[half][c], R=[t_mixfull[ch_ // 4]], W=[t_mfs[half][c]] + extra)

    def load(tb):
        xt, txt = p.xt[tb % 2], p.t_xt[tb % 2]
        k, off = tb // 4, (tb % 4) * 128
        src_ap, src_tok = xin_piece(k, off)
        s.dma("sp", xt[:, 0:HD], src_ap, txt, R=([src_tok] if src_tok is not None else []), W=[txt])

    load(0)
    for tb in range(NTB):
        half, tbh = tb // 8, tb % 8
        mf, t_mf = mfs[half], t_mfs[half]
        xt, txt = p.xt[tb % 2], p.t_xt[tb % 2]
        k, off = tb // 4, (tb % 4) * 128
        if tb + 1 < NTB:
            load(tb + 1)
        for fc in range(2):
            b = p.next_bank()
            for c in range(16):
                s.op("pe", lambda e: e.matmul(p.B[b][:], lhsT=mf[:, c, tbh * 128:(tbh + 1) * 128], rhs=wo[:, c, fc * 512:(fc + 1) * 512],
                                              start=(c == 0), stop=(c == 15)),
                     R=[t_wo[c], t_mf[c]], W=[p.t_B[b]])
            s.op("dve", lambda e: e.tensor_tensor(out=xt[:, fc * 512:(fc + 1) * 512], in0=xt[:, fc * 512:(fc + 1) * 512], in1=p.B[b][:], op=ALU.add),
                 R=[p.t_B[b], txt], W=[txt])
        s.dma("sp", xop[k].ap()[off:off + 128, :], xt[:, 0:HD], txt, R=[txt], W=[t_xop[k]])
        if tb % 4 == 3:
            s.coll(lambda e: e.collective_compute("AllGather", ALU.bypass, replica_groups=PAIRS,
                                                  ins=[xop[k].ap().opt()], outs=[xg[k].ap().opt()]),
                   R=[t_xop[k]], W=[t_xg[k]])
    s.op("dve", lambda e: e.memset(p.sm[:, 20:21], 0.0), W=alias + [p.t_sm])


def emit_final_norm(p, xg, t_xg, fg_ap, y_ap):
    s, nc = p.s, p.nc
    yts = [nc.alloc_sbuf_tensor_at(f"ytf{i}", [128, D], F32, offset=p.hT_off + i * 8192) for i in range(2)]
    t_yts = s.toks("ytf", 2)
    p.final_toks = s.toks("yout", 2)
    s.op("dve", lambda e: e.memset(p.sm[:, 21:22], 0.0), W=[p.t_hT, p.t_sm] + t_yts)
    s.dma("sp", p.gbc[:], fg_ap, p.t_gbc, W=[p.t_gbc])

    def load(tb):
        xt, txt = p.xt[tb % 2], p.t_xt[tb % 2]
        k, off = tb // 4, (tb % 4) * 128
        for r in range(2):
            s.dma("sp", xt[:, r * HD:(r + 1) * HD], xg[k].ap()[r * 512 + off:r * 512 + off + 128, :], txt, R=[t_xg[k]], W=[txt])

    load(0)
    for tb in range(NTB):
        i = tb % 2
        xt, txt = p.xt[i], p.t_xt[i]
        yt, tyt = yts[i], t_yts[i]
        if tb + 1 < NTB:
            load(tb + 1)
        st, tst = p.sm[:, 24 + 4 * i:28 + 4 * i], p.t_st[i]
        s.op("act", lambda e: e.activation(out=p.junk2[:], in_=xt[:], func=AF.Square, accum_out=st[:, 0:1]),
             R=[txt], W=[p.t_ft[0], p.t_ft[1], tst])
        s.op("act", lambda e: e.activation(out=st[:, 1:2], in_=st[:, 0:1], func=AF.Sqrt, bias=p.epsc[:, 0:1], scale=1.0 / D),
             R=[tst, p.t_const], W=[tst])
        s.op("dve", lambda e: e.reciprocal(out=st[:, 2:3], in_=st[:, 1:2]), R=[tst], W=[tst])
        s.op("dve", lambda e: e.scalar_tensor_tensor(out=yt[:], in0=xt[:], scalar=st[:, 2:3], in1=p.gbc[:], op0=ALU.mult, op1=ALU.mult),
             R=[txt, tst, p.t_gbc], W=[tyt])
        s.dma("sp", y_ap[tb * 128:(tb + 1) * 128, :], yt[:], tyt, R=[tyt], W=[p.final_toks[i]])


def build_fused2(depth=DEPTH):
    nc = bass.Bass("TRN2", target_bir_lowering=False)
    aps = {n: _dram(nc, n, ([depth] + list(shp[1:])) if (len(shp) > 2 and shp[0] == DEPTH and n != "rope") else shp) for n, shp in F2_INPUTS}
    y = _dram(nc, "y", [S, D], kind="ExternalOutput")
    mo = [nc.dram_tensor(f"mixown{i}", [4 * 128, S], BF16) for i in range(2)]
    mfu = [nc.dram_tensor(f"mixfull{i}", [2 * 4 * 128, S], BF16) for i in range(2)]
    xop = [[nc.dram_tensor(f"xop{q}_{k}", [512, HD], F32) for k in range(4)] for q in range(2)]
    xg = [nc.dram_tensor(f"xg{k}", [2 * 512, HD], F32) for k in range(4)]
    p = Prog(nc)
    s = p.s
    t_xop = [s.toks(f"xop{q}_", 4) for q in range(2)]
    t_xg = s.toks("xg", 4)
    t_mo, t_mfull = s.toks("mixown", 2), s.toks("mixfull", 2)
    p.wo = nc.alloc_sbuf_tensor_at("wo", [128, 16, HD], BF16, offset=p.hT_off)
    p.t_wo = s.toks("wo", 16)
    p.t_mf = s.toks("mf", 16)
    p.load_consts(aps["consts"])

    def x_from_xg(tb):
        k, off = tb // 4, (tb % 4) * 128
        return [(slice(r * HD, (r + 1) * HD), xg[k].ap()[r * 512 + off:r * 512 + off + 128, :], t_xg[k]) for r in range(2)]

    for l in range(depth):
        W = {k: aps[k][l] for k in ("wblk", "wv", "wf", "gbc", "fb", "lam", "lsc", "dng", "gng", "gba", "wa2")}

        def hook(g):
            for ch in (2 * g, 2 * g + 1):
                hs = ch // 4
                s.dma("sp", mo[hs].ap()[(ch % 4) * 128:(ch % 4 + 1) * 128, :], p.mixT[:, ch, :], p.t_mix[ch], R=[p.t_mix[ch]], W=[t_mo[hs]])
            if g % 2 == 1:
                hs = g // 2
                s.coll(lambda e: e.collective_compute("AllGather", ALU.bypass, replica_groups=PAIRS,
                                                      ins=[mo[hs].ap().opt()], outs=[mfu[hs].ap().opt()]),
                       R=[t_mo[hs]], W=[t_mfull[hs]])

        def pre_b(l=l):
            for c in range(16):
                s.dma("pool", p.wo[:, c, :], aps["wout"][l][c], p.t_wo[c], W=[p.t_wo[c]] + ([p.t_hT] if c == 0 else []))
        p.pre_b = pre_b
        p.program_a(aps["x"] if l == 0 else x_from_xg, W, aps["rope"], None, mixd=hook)
        q = l % 2
        if l == 0:
            xin = lambda k, off: (aps["xh"][k * 512 + off:k * 512 + off + 128, :], None)
        else:
            xin = lambda k, off, q=q: (xop[1 - q][k].ap()[off:off + 128, :], t_xop[1 - q][k])
        emit_b_fs(p, l, xin, None, mfu, t_mfull, xop[q], t_xop[q], xg, t_xg)
    emit_final_norm(p, xg, t_xg, aps["fg"], y)
    p.s.finish(p.final_toks)
    return nc, p


def kernel(x, norm_g, w_in, fox_fb, diff_lam, diff_norm_g, gla_wa2, gla_ba, gla_norm_g, w_out, final_norm_g):
    inputs = dict(x=np.asarray(x, np.float32), norm_g=np.asarray(norm_g, np.float32), w_in=np.asarray(w_in, np.float32),
                  fox_fb=np.asarray(fox_fb, np.float32), diff_lam=np.asarray(diff_lam, np.float32),
                  diff_norm_g=np.asarray(diff_norm_g, np.float32), gla_wa2=np.asarray(gla_wa2, np.float32),
                  gla_ba=np.asarray(gla_ba, np.float32), gla_norm_g=np.asarray(gla_norm_g, np.float32))
    w_out = np.asarray(w_out, np.float32)
    rope, consts = rope_tables(), const_tables()
    if "f2" not in _CACHE:
        _CACHE["f2"] = build_fused2()[0]
    nc = _CACHE["f2"]
    fg = np.ascontiguousarray(np.broadcast_to(np.asarray(final_norm_g, np.float32).reshape(1, D), (128, D)))
    wo_all = np.stack([wout_perm(w_out[l]) for l in range(DEPTH)])
    packs = []
    for hh in range(2):
        per_l = [layer_inputs_a(inputs, l, 0, hh, rope, consts) for l in range(DEPTH)]
        pk = {k: np.stack([per_l[l][k] for l in range(DEPTH)]) for k in ("wblk", "wv", "wf", "gbc", "fb", "lam", "lsc", "dng", "gng", "gba", "wa2")}
        pk["wout"] = np.ascontiguousarray(wo_all[:, :, :, hh * HD:(hh + 1) * HD])
        pk["fg"] = fg
        packs.append(pk)
    cores = list(range(8))
    maps = []
    for c in cores:
        b, hh = c // 2, c % 2
        m = dict(packs[hh])
        m["x"] = np.ascontiguousarray(inputs["x"][b])
        m["xh"] = np.ascontiguousarray(inputs["x"][b][:, hh * HD:(hh + 1) * HD])
        m["rope"] = rope
        m["consts"] = consts
        maps.append(m)
    res = run_bass_kernel_spmd(nc, maps, core_ids=cores)
    out = np.stack([res.results[2 * b]["y"] for b in range(inputs["x"].shape[0])])
    return out.astype(np.float32)
```
